# Optimizing a Trainium2 kernel written in Bass

```python
import math
import jax, jax.numpy as jnp
from jax import lax
import numpy as np

D_MODEL = 2048
BATCH = 1
SEQ = 8192
DEPTH = 2

NUM_MIXERS = 2
N_GLA_LAYERS = (DEPTH + NUM_MIXERS - 1) // NUM_MIXERS
N_GDN_LAYERS = DEPTH // NUM_MIXERS

EPS = 1e-6
CHUNK = 64
PLE_DIM = 256
D_FF = ((8 * D_MODEL + 3 * 256 - 1) // (3 * 256)) * 256

GLA_HEADS = 4
GLA_DK = D_MODEL // (2 * GLA_HEADS)
GLA_DV = D_MODEL // GLA_HEADS
GLA_KEY_DIM = GLA_HEADS * GLA_DK
GLA_VAL_DIM = GLA_HEADS * GLA_DV
GLA_GATE_RANK = 16
GLA_GATE_NORM = 16.0
GLA_PROJ = 2 * GLA_KEY_DIM + 2 * GLA_VAL_DIM + 2 * GLA_GATE_RANK

GDN_QK_HEADS = D_MODEL // 128
GDN_V_HEADS = 2 * GDN_QK_HEADS
GDN_DK = 128
GDN_DV = 128
GDN_KEY_DIM = GDN_QK_HEADS * GDN_DK
GDN_VAL_DIM = GDN_V_HEADS * GDN_DV
GDN_CONV_DIM = 2 * GDN_KEY_DIM + GDN_VAL_DIM
CONV_W = 5
GDN_PROJ = GDN_CONV_DIM + GDN_VAL_DIM + 4 * GDN_V_HEADS

kernel_name = "hybrid_gla_gdn_encoder"


def rmsnorm(x, w):
    xf = x.astype(jnp.float32)
    y = xf * lax.rsqrt(jnp.mean(xf * xf, axis=-1, keepdims=True) + EPS)
    return (y * w.astype(jnp.float32)).astype(x.dtype)


def l2norm(x):
    xf = x.astype(jnp.float32)
    return xf * lax.rsqrt(jnp.sum(xf * xf, axis=-1, keepdims=True) + EPS)


def to_heads(x, n):
    b, t, _ = x.shape
    return x.reshape(b, t, n, -1).transpose(0, 2, 1, 3)


def from_heads(x):
    b, n, t, d = x.shape
    return x.transpose(0, 2, 1, 3).reshape(b, t, n * d)


def flip_t(x):
    return jnp.flip(x, axis=2)


def gla_chunked(q, k, v, g):
    b, h, t, dk = q.shape
    dv = v.shape[-1]
    nc = t // CHUNK
    q, k, g = (a.reshape(b, h, nc, CHUNK, dk) for a in (q, k, g))
    v = v.reshape(b, h, nc, CHUNK, dv)
    cum = jnp.cumsum(g, axis=-2)
    last = cum[..., -1:, :]
    q_dec = q * jnp.exp(cum)
    k_dec = k * jnp.exp(last - cum)
    chunk_decay = jnp.exp(last[..., 0, :])
    lower = jnp.tril(jnp.ones((CHUNK, CHUNK), dtype=bool))
    xs = tuple(jnp.moveaxis(a, 2, 0) for a in (q, k, v, cum, q_dec, k_dec, chunk_decay))

    def step(S, inp):
        q_i, k_i, v_i, c_i, qd_i, kd_i, cd_i = inp
        pair = jnp.where(lower[..., None],
                         jnp.exp(jnp.minimum(c_i[..., :, None, :] - c_i[..., None, :, :], 0.0)), 0.0)
        a_i = jnp.einsum('bhid,bhjd,bhijd->bhij', q_i, k_i, pair)
        o_i = jnp.einsum('bhij,bhjv->bhiv', a_i, v_i) + jnp.einsum('bhid,bhdv->bhiv', qd_i, S)
        S = S * cd_i[..., None] + jnp.einsum('bhcd,bhcv->bhdv', kd_i, v_i)
        return S, o_i

    S0 = jnp.zeros((b, h, dk, dv), jnp.float32)
    _, o = lax.scan(step, S0, xs)
    return jnp.moveaxis(o, 0, 2).reshape(b, h, t, dv)


def gated_delta_chunked(q, k, v, g, beta):
    b, h, t, dk = q.shape
    dv = v.shape[-1]
    nc = t // CHUNK
    q = q.reshape(b, h, nc, CHUNK, dk)
    k = k.reshape(b, h, nc, CHUNK, dk)
    v = v.reshape(b, h, nc, CHUNK, dv)
    cum = jnp.cumsum(g.reshape(b, h, nc, CHUNK), axis=-1)
    beta = beta.reshape(b, h, nc, CHUNK, 1)
    idx = jnp.arange(CHUNK)
    lower = idx[:, None] >= idx[None, :]
    strict = idx[:, None] > idx[None, :]
    decay = jnp.where(lower, jnp.exp(jnp.minimum(cum[..., :, None] - cum[..., None, :], 0.0)), 0.0)
    k_beta = k * beta
    tri = jnp.where(strict, jnp.einsum('bhnid,bhnjd->bhnij', k_beta, k) * decay, 0.0) \
        + jnp.eye(CHUNK, dtype=jnp.float32)
    rhs = jnp.concatenate([v * beta, k_beta * jnp.exp(cum)[..., None]], axis=-1)
    sol = lax.linalg.triangular_solve(tri, rhs, left_side=True, lower=True, unit_diagonal=True)
    u, w = sol[..., :dv], sol[..., dv:]
    attn = jnp.einsum('bhnid,bhnjd->bhnij', q, k) * decay
    last = cum[..., -1:]
    q_dec = q * jnp.exp(cum)[..., None]
    k_dec = k * jnp.exp(last - cum)[..., None]
    chunk_decay = jnp.exp(last)
    xs = tuple(jnp.moveaxis(a, 2, 0) for a in (u, w, attn, q_dec, k_dec, chunk_decay))

    def step(S, inp):
        u_i, w_i, a_i, qd_i, kd_i, cd_i = inp
        v_new = u_i - jnp.einsum('bhcd,bhdv->bhcv', w_i, S)
        o_i = jnp.einsum('bhcd,bhdv->bhcv', qd_i, S) + jnp.einsum('bhij,bhjv->bhiv', a_i, v_new)
        S = S * cd_i[..., None] + jnp.einsum('bhcd,bhcv->bhdv', kd_i, v_new)
        return S, o_i

    S0 = jnp.zeros((b, h, dk, dv), jnp.float32)
    _, o = lax.scan(step, S0, xs)
    return jnp.moveaxis(o, 0, 2).reshape(b, h, t, dv)


def centred_short_conv(x, w):
    c = x.shape[-1]
    return lax.conv_general_dilated(
        x, w[:, None, :].astype(x.dtype), window_strides=(1,),
        padding=[(CONV_W // 2, CONV_W // 2)],
        dimension_numbers=('NWC', 'WIO', 'NWC'), feature_group_count=c)


def gla_mixer(hn, w_in, w_gate_up, b_gate, out_norm, w_out):
    b, t, _ = hn.shape
    proj = jnp.einsum('btd,de->bte', hn, w_in)
    q, k, v, og, lr = jnp.split(
        proj, [GLA_KEY_DIM, 2 * GLA_KEY_DIM, 2 * GLA_KEY_DIM + GLA_VAL_DIM,
               2 * GLA_KEY_DIM + 2 * GLA_VAL_DIM], axis=-1)
    lr = lr.reshape(b, t, 2, GLA_GATE_RANK)
    gk = jnp.einsum('btzr,zrd->zbtd', lr, w_gate_up) + b_gate[:, None, None, :]
    logdec = jax.nn.log_sigmoid(gk.astype(jnp.float32)) / GLA_GATE_NORM
    qh = to_heads(q, GLA_HEADS).astype(jnp.float32) * (GLA_DK ** -0.5)
    kh = to_heads(k, GLA_HEADS).astype(jnp.float32)
    vh = to_heads(v, GLA_HEADS).astype(jnp.float32)
    g_fwd = to_heads(logdec[0], GLA_HEADS)
    g_bwd = to_heads(logdec[1], GLA_HEADS)
    o = gla_chunked(qh, kh, vh, g_fwd) \
        + flip_t(gla_chunked(flip_t(qh), flip_t(kh), flip_t(vh), flip_t(g_bwd)))
    o = rmsnorm(o, out_norm)
    o = from_heads(o).astype(hn.dtype) * jax.nn.silu(og)
    return jnp.einsum('bte,ed->btd', o, w_out)


def gdn_mixer(hn, w_in, conv_w, a_log, dt_bias, out_norm, w_out):
    b, t, _ = hn.shape
    proj = jnp.einsum('btd,de->bte', hn, w_in)
    qkv, z, ba = jnp.split(proj, [GDN_CONV_DIM, GDN_CONV_DIM + GDN_VAL_DIM], axis=-1)
    qkv = jax.nn.silu(centred_short_conv(qkv, conv_w))
    q, k, v = jnp.split(qkv, [GDN_KEY_DIM, 2 * GDN_KEY_DIM], axis=-1)
    rep = GDN_V_HEADS // GDN_QK_HEADS
    qh = jnp.repeat(l2norm(to_heads(q, GDN_QK_HEADS)), rep, axis=1) * (GDN_DK ** -0.5)
    kh = jnp.repeat(l2norm(to_heads(k, GDN_QK_HEADS)), rep, axis=1)
    vh = to_heads(v, GDN_V_HEADS).astype(jnp.float32)
    ba = ba.reshape(b, t, 2, 2, GDN_V_HEADS).astype(jnp.float32)
    beta = jax.nn.sigmoid(ba[:, :, 0]).transpose(2, 0, 3, 1)
    a_in = ba[:, :, 1].transpose(2, 0, 3, 1)
    g = -jnp.exp(a_log.astype(jnp.float32))[:, None, :, None] \
        * jax.nn.softplus(a_in + dt_bias.astype(jnp.float32)[:, None, :, None])
    o = gated_delta_chunked(qh, kh, vh, g[0], beta[0]) \
        + flip_t(gated_delta_chunked(flip_t(qh), flip_t(kh), flip_t(vh), flip_t(g[1]), flip_t(beta[1])))
    o = rmsnorm(o, out_norm)
    o = from_heads(o).astype(hn.dtype) * jax.nn.silu(z)
    return jnp.einsum('bte,ed->btd', o, w_out)


def swiglu(hn, w_in, w_out):
    gate, up = jnp.split(jnp.einsum('btd,de->bte', hn, w_in), 2, axis=-1)
    return jnp.einsum('btf,fd->btd', jax.nn.silu(gate) * up, w_out)


def setup_inputs(seed: int = 0) -> dict:
    key = jax.random.key(seed)
    ks = jax.random.split(key, 21)
    f32 = jnp.float32

    def nrm(k, shape, scale):
        return jax.random.normal(k, shape, f32) * scale

    def gain(k, shape):
        return 1.0 + 0.05 * jax.random.normal(k, shape, f32)

    x = nrm(ks[0], (BATCH, SEQ, D_MODEL), 1.0)
    p = nrm(ks[1], (DEPTH, BATCH, SEQ, PLE_DIM), 1.0)
    mixer_norm = gain(ks[2], (DEPTH, D_MODEL))
    gla_w_in = nrm(ks[3], (N_GLA_LAYERS, D_MODEL, GLA_PROJ), D_MODEL ** -0.5)
    gla_w_gate_up = nrm(ks[4], (N_GLA_LAYERS, 2, GLA_GATE_RANK, GLA_KEY_DIM), GLA_GATE_RANK ** -0.5)
    gla_b_gate = nrm(ks[5], (N_GLA_LAYERS, 2, GLA_KEY_DIM), 0.5)
    gla_out_norm = gain(ks[6], (N_GLA_LAYERS, GLA_DV))
    gla_w_out = nrm(ks[7], (N_GLA_LAYERS, GLA_VAL_DIM, D_MODEL), GLA_VAL_DIM ** -0.5)
    gdn_w_in = nrm(ks[8], (N_GDN_LAYERS, D_MODEL, GDN_PROJ), D_MODEL ** -0.5)
    gdn_conv = nrm(ks[9], (N_GDN_LAYERS, CONV_W, GDN_CONV_DIM), CONV_W ** -0.5)
    gdn_a_log = jnp.log(jax.random.uniform(ks[10], (N_GDN_LAYERS, 2, GDN_V_HEADS), f32, 1.0, 16.0))
    dt = jnp.exp(jax.random.uniform(ks[11], (N_GDN_LAYERS, 2, GDN_V_HEADS), f32,
                                    math.log(1e-3), math.log(1e-1)))
    gdn_dt_bias = dt + jnp.log(-jnp.expm1(-dt))
    gdn_out_norm = gain(ks[12], (N_GDN_LAYERS, GDN_DV))
    gdn_w_out = nrm(ks[13], (N_GDN_LAYERS, GDN_VAL_DIM, D_MODEL), GDN_VAL_DIM ** -0.5)
    ffn_norm = gain(ks[14], (DEPTH, D_MODEL))
    ffn_w_in = nrm(ks[15], (DEPTH, D_MODEL, 2 * D_FF), D_MODEL ** -0.5)
    ffn_w_out = nrm(ks[16], (DEPTH, D_FF, D_MODEL), D_FF ** -0.5)
    ple_norm = gain(ks[17], (DEPTH, D_MODEL))
    ple_w_gate = nrm(ks[18], (DEPTH, D_MODEL, D_MODEL), D_MODEL ** -0.5)
    ple_w_proj = nrm(ks[19], (DEPTH, PLE_DIM, D_MODEL), PLE_DIM ** -0.5)
    final_norm = gain(ks[20], (D_MODEL,))
    return {"x": x, "p": p, "mixer_norm": mixer_norm,
            "gla_w_in": gla_w_in, "gla_w_gate_up": gla_w_gate_up, "gla_b_gate": gla_b_gate,
            "gla_out_norm": gla_out_norm, "gla_w_out": gla_w_out,
            "gdn_w_in": gdn_w_in, "gdn_conv": gdn_conv, "gdn_a_log": gdn_a_log,
            "gdn_dt_bias": gdn_dt_bias, "gdn_out_norm": gdn_out_norm, "gdn_w_out": gdn_w_out,
            "ffn_norm": ffn_norm, "ffn_w_in": ffn_w_in, "ffn_w_out": ffn_w_out,
            "ple_norm": ple_norm, "ple_w_gate": ple_w_gate, "ple_w_proj": ple_w_proj,
            "final_norm": final_norm}


def reference(x, p, mixer_norm, gla_w_in, gla_w_gate_up, gla_b_gate, gla_out_norm, gla_w_out,
              gdn_w_in, gdn_conv, gdn_a_log, gdn_dt_bias, gdn_out_norm, gdn_w_out,
              ffn_norm, ffn_w_in, ffn_w_out, ple_norm, ple_w_gate, ple_w_proj, final_norm):
    h = x
    for i in range(DEPTH):
        j = i // NUM_MIXERS
        hn = rmsnorm(h, mixer_norm[i])
        if i % NUM_MIXERS == 0:
            mix = gla_mixer(hn, gla_w_in[j], gla_w_gate_up[j], gla_b_gate[j],
                            gla_out_norm[j], gla_w_out[j])
        else:
            mix = gdn_mixer(hn, gdn_w_in[j], gdn_conv[j], gdn_a_log[j], gdn_dt_bias[j],
                            gdn_out_norm[j], gdn_w_out[j])
        h = h + mix
        h = h + swiglu(rmsnorm(h, ffn_norm[i]), ffn_w_in[i], ffn_w_out[i])
        gate = jax.nn.sigmoid(jnp.einsum('btd,de->bte', rmsnorm(h, ple_norm[i]), ple_w_gate[i]))
        h = h + gate * jnp.einsum('btk,kd->btd', p[i], ple_w_proj[i])
    return rmsnorm(h, final_norm)
```

```python
import contextlib
import numpy as np
import ml_dtypes
import concourse.bass as bass
import concourse.mybir as mybir
from concourse.bass_utils import run_bass_kernel_spmd

F32 = mybir.dt.float32
BF16 = mybir.dt.bfloat16
AF = mybir.ActivationFunctionType
ALU = mybir.AluOpType

NCORES = 8
T = 8192
D = 2048
TL = T // NCORES
EPS = 1e-6
D_FF = 5632
PLE = 256
GLA_H, GLA_DK, GLA_DV, GLA_R = 4, 256, 512, 16
GLA_KD, GLA_VD = 1024, 2048
GLA_PROJ = 2 * GLA_KD + 2 * GLA_VD + 2 * GLA_R
GDN_QKH, GDN_VH, GDN_DK, GDN_DV = 16, 32, 128, 128
GDN_KD, GDN_VD = 2048, 4096
GDN_CONV = 2 * GDN_KD + GDN_VD
GDN_PROJ = GDN_CONV + GDN_VD + 4 * GDN_VH
CONV_W = 5

SEM_EPOCH = 20000


class TT:
    __slots__ = ("name", "last_w", "readers", "sems", "cnts", "excl")

    def __init__(self, name):
        self.name = name
        self.excl = False
        self.last_w = None
        self.readers = []
        self.sems = {}
        self.cnts = {}


class Op:
    __slots__ = ("eng", "fn", "deps", "needs_inc", "sem", "val", "is_dma", "idx")

    def __init__(self, eng, fn, is_dma):
        self.eng = eng
        self.fn = fn
        self.deps = []
        self.needs_inc = False
        self.sem = None
        self.val = 0
        self.is_dma = is_dma
        self.idx = 0


COMPUTE = ("pe", "act", "dve", "pool")


class Sched:
    def __init__(self, nc, stack):
        self.nc = nc
        self.stack = stack
        self.ops = {"pe": [], "act": [], "dve": [], "pool": [], "sp": []}
        self.nops = 0
        self.dma_tiles = []

    def tile(self, name):
        return TT(name)

    def _sem(self, name):
        return self.stack.enter_context(self.nc.semaphore(name))

    def _deps(self, op, reads, writes):
        ex = [r for r in reads if r.excl and r not in writes]
        if ex:
            reads = [r for r in reads if not r.excl]
            writes = list(writes) + ex
        cand = []
        for r in reads:
            if r.last_w is not None:
                cand.append(r.last_w)
        for w in writes:
            if w.last_w is not None:
                cand.append(w.last_w)
            cand.extend(w.readers)
        best = {}
        for d in cand:
            if d is op:
                continue
            if d.is_dma:
                key = ("dma", id(d.sem))
                if key not in best or best[key].val < d.val:
                    best[key] = d
            else:
                if d.eng == "pe" and op.eng == "pe" and not op.is_dma:
                    continue
                key = d.eng
                if key not in best or best[key].idx < d.idx:
                    best[key] = d
        for d in best.values():
            if not d.is_dma:
                d.needs_inc = True
            op.deps.append(d)
        for w in writes:
            w.last_w = op
            w.readers = []
        for r in reads:
            if r.last_w is not op:
                r.readers.append(op)

    def op(self, eng, fn, reads=(), writes=()):
        o = Op(eng, fn, False)
        o.idx = self.nops
        self.nops += 1
        self._deps(o, reads, writes)
        self.ops[eng].append(o)
        return o

    def dma(self, eng, fn, tile, kind, reads=(), writes=(), n=1):
        o = Op(eng, fn, True)
        o.idx = self.nops
        self.nops += 1
        if kind not in tile.sems:
            tile.sems[kind] = self._sem(f"d_{tile.name}_{kind}"[:40])
            tile.cnts[kind] = 0
            self.dma_tiles.append((tile, kind))
        self._deps(o, reads, writes)
        tile.cnts[kind] += 16 * n
        o.sem = tile.sems[kind]
        o.val = tile.cnts[kind]
        self.ops[eng].append(o)
        return o

    def emit(self):
        nc = self.nc
        for e in COMPUTE:
            cur, cnt, k = None, 0, 0
            for o in self.ops[e]:
                if o.is_dma or not o.needs_inc:
                    continue
                if cur is None or cnt >= SEM_EPOCH:
                    cur = self._sem(f"s_{e}_{k}")
                    k += 1
                    cnt = 0
                cnt += 1
                o.sem, o.val = cur, cnt
        finals = [(t.sems[k], t.cnts[k]) for (t, k) in self.dma_tiles]
        ops = self.ops

        def run(engname, eng, final=False):
            waited = {}
            for o in ops[engname]:
                for d in o.deps:
                    key = id(d.sem)
                    if waited.get(key, 0) >= d.val:
                        continue
                    eng.wait_ge(d.sem, d.val)
                    waited[key] = d.val
                if o.is_dma:
                    o.fn(eng, o.sem)
                else:
                    ins = o.fn(eng)
                    if o.needs_inc:
                        ins.then_inc(o.sem, 1)
            if final:
                for sem, val in finals:
                    if waited.get(id(sem), 0) < val:
                        eng.wait_ge(sem, val)

        with nc.Block() as block:
            @block.sync
            def _(e):
                run("sp", e, final=True)

            @block.tensor
            def _(e):
                run("pe", e)

            @block.scalar
            def _(e):
                run("act", e)

            @block.vector
            def _(e):
                run("dve", e)

            @block.gpsimd
            def _(e):
                run("pool", e)


class Ctx:
    def __init__(self):
        self.nc = bass.Bass("TRN2", target_bir_lowering=False)
        self.stack = contextlib.ExitStack()
        self.s = Sched(self.nc, self.stack)
        self.n = 0

    def dram_in(self, name, shape, dt=F32):
        return self.nc.dram_tensor(name, list(shape), dt, kind="ExternalInput").ap()

    def dram_out(self, name, shape, dt=F32):
        return self.nc.dram_tensor(name, list(shape), dt, kind="ExternalOutput").ap()

    def sb(self, shape, dt, name=None):
        self.n += 1
        name = "S_" + (name or f"sb{self.n}")
        t = self.stack.enter_context(self.nc.sbuf_tensor(name, list(shape), dt))
        return t, self.s.tile(name)

    def ps(self, shape, dt, name=None):
        self.n += 1
        name = "P_" + (name or f"ps{self.n}")
        t = self.stack.enter_context(self.nc.psum_tensor(name, list(shape), dt))
        tt = self.s.tile(name)
        tt.excl = True
        return t, tt

    def load(self, eng, dst_ap, src_ap, tile):
        self.s.dma(eng, lambda e, sem: e.dma_start(out=dst_ap, in_=src_ap).then_inc(sem, 16),
                   tile, "ld", writes=[tile])

    def store(self, eng, dst_ap, src_ap, tile):
        self.s.dma(eng, lambda e, sem: e.dma_start(out=dst_ap, in_=src_ap).then_inc(sem, 16),
                   tile, "st", reads=[tile])

    def finish(self):
        self.s.emit()
        self.stack.close()
        return self.nc


class Rot:
    def __init__(self, items):
        self.items = items
        self.i = 0

    def next(self):
        it = self.items[self.i % len(self.items)]
        self.i += 1
        return it


NT = TL // 128
KD = D // 128


class Dense:
    def __init__(self, cx, ident_d):
        self.cx = cx
        nc = cx.nc
        s = cx.s
        self.h, _ = cx.sb([128, NT, D], F32, "h")
        self.h_t = [s.tile(f"h{i}") for i in range(NT)]
        self.aT, _ = cx.sb([128, KD, TL], BF16, "aT")
        self.aT_t = [s.tile(f"aT{i}") for i in range(NT)]
        self.ident, self.ident_t = cx.sb([128, 128], BF16, "ident")
        cx.load("sp", self.ident[:], ident_d[:, :], self.ident_t)
        self.consts, self.consts_t = cx.sb([128, 4], F32, "consts")
        cx.s.op("dve", lambda e: e.memset(self.consts[:, 0:1], EPS), writes=[self.consts_t])
        self.xn = Rot([cx.sb([128, D], BF16, f"xn{i}") for i in range(1)])
        self.junk = Rot([cx.sb([128, D], BF16, f"junk{i}") for i in range(1)])
        self.gains = Rot([cx.sb([128, D], F32, f"gain{i}") for i in range(2)])
        self.stat = Rot([cx.sb([128, 8], F32, f"stat{i}") for i in range(4)])
        self.pst = Rot([cx.ps([128, 1024], BF16, f"pst{i}") for i in range(2)])
        self.psm = Rot([cx.ps([128, 512], F32, f"psm{i}") for i in range(5)])
        self.wp = Rot([cx.sb([128, 16 * 512], BF16, f"wp{i}") for i in range(3)])
        self.ev = Rot([cx.sb([128, 512], F32, f"ev{i}") for i in range(3)])

    def load_w(self, w_d, r0, kc, c0, ew):
        cx = self.cx
        wt, wtile = self.wp.next()
        dst = wt[:, 0:kc * ew].rearrange("p (k e) -> p k e", e=ew)
        nsplit = max(1, (kc + 7) // 8)
        per = (kc + nsplit - 1) // nsplit

        def fn(e, sem):
            for j in range(nsplit):
                k0, k1 = j * per, min(kc, (j + 1) * per)
                src = w_d[r0 + k0 * 128:r0 + k1 * 128, c0:c0 + ew].rearrange("(k p) e -> p k e", p=128)
                e.dma_start(out=dst[:, k0:k1, :], in_=src).then_inc(sem, 16)
        cx.s.dma("pool", fn, wtile, "ld", writes=[wtile], n=nsplit)
        return dst, wtile

    def rstd(self, src_ap, src_tiles, width, eps=EPS, mean=True):
        cx = self.cx
        junk, junk_t = self.junk.next()
        st, st_t = self.stat.next()
        eps_ap = self.consts[:, 0:1] if eps == EPS else float(eps)
        cx.s.op("act", lambda e: e.activation(out=junk[:, 0:width], in_=src_ap, func=AF.Square,
                                              accum_out=st[:, 0:1]),
                reads=list(src_tiles), writes=[junk_t, st_t])
        n = float(width) if mean else 1.0
        cx.s.op("act", lambda e: e.activation(out=st[:, 1:2], in_=st[:, 0:1], func=AF.Sqrt, bias=eps_ap,
                                              scale=1.0 / n), reads=[st_t, self.consts_t], writes=[st_t])
        cx.s.op("dve", lambda e: e.reciprocal(out=st[:, 2:3], in_=st[:, 1:2]), reads=[st_t], writes=[st_t])
        return st[:, 2:3], st_t

    def to_feat(self, xn, xn_t, i, ncols, k0=0):
        cx = self.cx
        nk = ncols // 128
        for g in range(0, nk, 8):
            ng = min(8, nk - g)
            pt, pt_t = self.pst.next()
            for j in range(ng):
                c = (g + j) * 128
                cx.s.op("pe", lambda e, j=j, c=c, pt=pt: e.transpose(out=pt[:, j * 128:(j + 1) * 128],
                                                                      in_=xn[:, c:c + 128],
                                                                      identity=self.ident[:]),
                        reads=[xn_t, self.ident_t], writes=[pt_t])
            dst = self.aT[:, k0 + g:k0 + g + ng, i * 128:(i + 1) * 128]
            src = pt[:, 0:ng * 128].rearrange("p (k t) -> p k t", t=128)
            eng = "act" if (g // 8) % 2 == 0 else "dve"
            if eng == "act":
                cx.s.op("act", lambda e, dst=dst, src=src: e.copy(out=dst, in_=src),
                        reads=[pt_t], writes=[self.aT_t[i]])
            else:
                cx.s.op("dve", lambda e, dst=dst, src=src: e.tensor_copy(out=dst, in_=src),
                        reads=[pt_t], writes=[self.aT_t[i]])

    def norm_to_feat(self, w_bc, w_bc_t):
        cx = self.cx
        for i in range(NT):
            r, r_t = self.rstd(self.h[:, i, :], [self.h_t[i]], D)
            xn, xn_t = self.xn.next()
            cx.s.op("dve", lambda e, i=i, r=r, xn=xn: e.scalar_tensor_tensor(
                out=xn[:], in0=self.h[:, i, :], scalar=r, in1=w_bc[:], op0=ALU.mult, op1=ALU.mult),
                reads=[self.h_t[i], r_t, w_bc_t], writes=[xn_t])
            self.to_feat(xn, xn_t, i, D)

    def mm_tok(self, w_d, kc, c0, c1, evac, r0=0, ew=512, k0=0):
        cx = self.cx
        for c in range(c0, c1, ew):
            w = min(ew, c1 - c)
            wt, wtile = self.load_w(w_d, r0, kc, c, w)
            for i in range(NT):
                ps, ps_t = self.psm.next()
                for k in range(kc):
                    cx.s.op("pe", lambda e, k=k, i=i, ps=ps, wt=wt, w=w: e.matmul(
                        ps[:, 0:w], lhsT=self.aT[:, k0 + k, i * 128:(i + 1) * 128], rhs=wt[:, k, 0:w],
                        start=(k == 0), stop=(k == kc - 1)),
                        reads=[self.aT_t[i], wtile], writes=[ps_t])
                evac(i, c, w, ps, ps_t)

    def load_bc(self, src_d, n, name):
        t, tt = self.cx.sb([128, n], F32, name)
        self.cx.load("sp", t[:], src_d[:, :], tt)
        return t, tt

    def load_h(self, x_d):
        for i in range(NT):
            self.cx.load("sp", self.h[:, i, :], x_d[i * 128:(i + 1) * 128, :], self.h_t[i])

    def proj_store(self, w_d, ncols, out_d):
        cx = self.cx

        def evac(i, c, w, ps, ps_t):
            ev, ev_t = self.ev.next()
            cx.s.op("act", lambda e: e.copy(out=ev[:, 0:w], in_=ps[:, 0:w]), reads=[ps_t], writes=[ev_t])
            cx.store("sp", out_d[i * 128:(i + 1) * 128, c:c + w], ev[:, 0:w], ev_t)
        self.mm_tok(w_d, KD, 0, ncols, evac)


def bf16_ident():
    return np.eye(128, dtype=np.float32).astype(ml_dtypes.bfloat16)


def rep128(v):
    v = np.asarray(v, np.float32).reshape(1, -1)
    return np.ascontiguousarray(np.broadcast_to(v, (128, v.shape[1])))


def build_phase_a(ncols):
    cx = Ctx()
    x_d = cx.dram_in("x", [TL, D])
    g_d = cx.dram_in("g", [128, D])
    w_d = cx.dram_in("w", [D, ncols])
    id_d = cx.dram_in("ident", [128, 128], BF16)
    out_d = cx.dram_out("proj", [TL, ncols])
    dn = Dense(cx, id_d)
    g, g_t = dn.load_bc(g_d, D, "g_bc")
    dn.load_h(x_d)
    dn.norm_to_feat(g, g_t)
    dn.proj_store(w_d, ncols, out_d)
    return cx.finish()


def run_phase_a(x, gain, w):
    ncols = w.shape[1]
    nc = build_phase_a(ncols)
    ident = bf16_ident()
    g = rep128(gain)
    in_maps = [{"x": np.ascontiguousarray(x[c * TL:(c + 1) * TL]), "g": g, "w": w, "ident": ident}
               for c in range(NCORES)]
    res = run_bass_kernel_spmd(nc, in_maps, core_ids=list(range(NCORES)))
    return np.concatenate([r["proj"] for r in res.results], axis=0)


GB = 512


def build_gla_scan(Tn=T):
    cx = Ctx()
    s = cx.s
    qT_d = cx.dram_in("qT", [GLA_DK, Tn])
    kT_d = cx.dram_in("kT", [GLA_DK, Tn])
    k_d = cx.dram_in("k", [Tn, GLA_DK])
    v_d = cx.dram_in("v", [Tn, GLA_DV])
    lr_d = cx.dram_in("lrT", [GLA_R, Tn])
    wg_d = cx.dram_in("wgu", [GLA_R, GLA_DK])
    bb_d = cx.dram_in("b_bc", [128, GLA_DK])
    cst_d = cx.dram_in("cst", [128, 3 * 128])
    o_d = cx.dram_out("o", [Tn, GLA_DV])

    cst, cst_t = cx.sb([128, 384], F32, "cst")
    cx.load("sp", cst[:], cst_d[:, :], cst_t)
    triS, triUS, maskU = cst[:, 0:128], cst[:, 128:256], cst[:, 256:384]
    wg, wg_t = cx.sb([GLA_R, GLA_DK], F32, "wg")
    cx.load("sp", wg[:], wg_d[:, :], wg_t)
    bb, bb_t = cx.sb([128, GLA_DK], F32, "bb")
    cx.load("sp", bb[:], bb_d[:, :], bb_t)

    nb = Tn // GB
    cpb = GB // 128
    qTb = Rot([cx.sb([128, 2, GB], F32, f"qTb{i}") for i in range(2)])
    kTb = Rot([cx.sb([128, 2, GB], F32, f"kTb{i}") for i in range(2)])
    kb = Rot([cx.sb([128, cpb, GLA_DK], BF16, f"kb{i}") for i in range(2)])
    vb = Rot([cx.sb([128, cpb, GLA_DV], BF16, f"vb{i}") for i in range(2)])
    lrb = Rot([cx.sb([GLA_R, GB], F32, f"lrb{i}") for i in range(2)])

    Sf, Sf_t = cx.sb([128, 2, GLA_DV], F32, "Sf")
    Sb, Sb_t = cx.sb([128, 2, GLA_DV], BF16, "Sb")
    s.op("dve", lambda e: e.memset(Sf[:], 0.0), writes=[Sf_t])
    s.op("pool", lambda e: e.memset(Sb[:], 0.0), writes=[Sb_t])

    ps = Rot([cx.ps([128, 512], F32, f"ps{i}") for i in range(8)])
    gk = Rot([cx.sb([128, GLA_DK], F32, f"gk{i}") for i in range(2)])
    sp = Rot([cx.sb([128, GLA_DK], F32, f"sp{i}") for i in range(2)])
    kds = Rot([cx.sb([128, GLA_DK], F32, f"kds{i}") for i in range(2)])
    kdec = Rot([cx.sb([128, GLA_DK], BF16, f"kdec{i}") for i in range(2)])
    eq = Rot([cx.sb([128, 2, 128], F32, f"eq{i}") for i in range(2)])
    en = Rot([cx.sb([128, 2, 128], F32, f"en{i}") for i in range(2)])
    qdT = Rot([cx.sb([128, 2, 128], BF16, f"qdT{i}") for i in range(2)])
    kiT = Rot([cx.sb([128, 2, 128], BF16, f"kiT{i}") for i in range(2)])
    AT = Rot([cx.sb([128, 128], BF16, f"AT{i}") for i in range(2)])
    osb = Rot([cx.sb([128, GLA_DV], F32, f"osb{i}") for i in range(3)])
    scale = float(GLA_DK) ** -0.5

    for b in range(nb):
        t0 = b * GB
        qt, qt_t = qTb.next()
        kt, kt_t = kTb.next()
        kk, kk_t = kb.next()
        vv, vv_t = vb.next()
        lr, lr_t = lrb.next()
        cx.load("sp", qt[:], qT_d[:, t0:t0 + GB].rearrange("(c p) t -> p c t", p=128), qt_t)
        cx.load("sp", kt[:], kT_d[:, t0:t0 + GB].rearrange("(c p) t -> p c t", p=128), kt_t)
        cx.load("pool", kk[:], k_d[t0:t0 + GB, :].rearrange("(c p) d -> p c d", p=128), kk_t)
        cx.load("pool", vv[:], v_d[t0:t0 + GB, :].rearrange("(c p) d -> p c d", p=128), vv_t)
        cx.load("sp", lr[:], lr_d[:, t0:t0 + GB], lr_t)
        for c in range(cpb):
            cs = slice(c * 128, (c + 1) * 128)
            p1, p1_t = ps.next()
            s.op("pe", lambda e, p1=p1, lr=lr, cs=cs: e.matmul(p1[:, 0:GLA_DK], lhsT=lr[:, cs], rhs=wg[:],
                                                                 start=True, stop=True),
                 reads=[lr_t, wg_t], writes=[p1_t])
            g1, g1_t = gk.next()
            s.op("dve", lambda e, g1=g1, p1=p1: e.tensor_tensor(out=g1[:], in0=p1[:, 0:GLA_DK], in1=bb[:],
                                                                 op=ALU.add),
                 reads=[p1_t, bb_t], writes=[g1_t])
            s1, s1_t = sp.next()
            s.op("act", lambda e, g1=g1: e.activation(out=g1[:], in_=g1[:], func=AF.Exp, scale=-1.0),
                 reads=[g1_t], writes=[g1_t])
            s.op("act", lambda e, g1=g1, s1=s1: e.activation(out=s1[:], in_=g1[:], func=AF.Ln, bias=1.0),
                 reads=[g1_t], writes=[s1_t])
            p2, p2_t = ps.next()
            s.op("pe", lambda e, p2=p2, s1=s1: e.matmul(p2[:, 0:GLA_DK], lhsT=triUS, rhs=s1[:], start=True,
                                                         stop=True),
                 reads=[cst_t, s1_t], writes=[p2_t])
            p3, p3_t = ps.next()
            for dc in range(2):
                s.op("pe", lambda e, p3=p3, s1=s1, dc=dc: e.matmul(
                    p3[:, dc * 128:(dc + 1) * 128], lhsT=s1[:, dc * 128:(dc + 1) * 128], rhs=triS,
                    start=True, stop=True), reads=[cst_t, s1_t], writes=[p3_t])
            kd, kd_t = kds.next()
            s.op("act", lambda e, kd=kd, p2=p2: e.activation(out=kd[:], in_=p2[:, 0:GLA_DK], func=AF.Exp),
                 reads=[p2_t], writes=[kd_t])
            kdc, kdc_t = kdec.next()
            s.op("pool", lambda e, kdc=kdc, kk=kk, c=c, kd=kd: e.tensor_tensor(out=kdc[:], in0=kk[:, c, :],
                                                                                in1=kd[:], op=ALU.mult),
                 reads=[kk_t, kd_t], writes=[kdc_t])
            e1, e1_t = eq.next()
            e2, e2_t = en.next()
            p3v = p3[:, 0:256].rearrange("p (c t) -> p c t", t=128)
            s.op("act", lambda e, e1=e1, p3v=p3v: e.activation(out=e1[:], in_=p3v, func=AF.Exp),
                 reads=[p3_t], writes=[e1_t])
            s.op("act", lambda e, e2=e2, p3v=p3v: e.activation(out=e2[:], in_=p3v, func=AF.Exp, scale=-1.0),
                 reads=[p3_t], writes=[e2_t])
            qd, qd_t = qdT.next()
            ki, ki_t = kiT.next()
            s.op("dve", lambda e, qd=qd, qt=qt, cs=cs, e1=e1: e.scalar_tensor_tensor(
                out=qd[:], in0=qt[:, :, cs], scalar=scale, in1=e1[:], op0=ALU.mult, op1=ALU.mult),
                reads=[qt_t, e1_t], writes=[qd_t])
            s.op("dve", lambda e, ki=ki, kt=kt, cs=cs, e2=e2: e.tensor_tensor(
                out=ki[:], in0=kt[:, :, cs], in1=e2[:], op=ALU.mult),
                reads=[kt_t, e2_t], writes=[ki_t])
            p4, p4_t = ps.next()
            for dc in range(2):
                s.op("pe", lambda e, p4=p4, ki=ki, qd=qd, dc=dc: e.matmul(
                    p4[:, 0:128], lhsT=ki[:, dc, :], rhs=qd[:, dc, :], start=(dc == 0), stop=(dc == 1)),
                    reads=[ki_t, qd_t], writes=[p4_t])
            at, at_t = AT.next()
            s.op("dve", lambda e, at=at, p4=p4: e.tensor_tensor(out=at[:], in0=p4[:, 0:128], in1=maskU,
                                                                 op=ALU.mult),
                 reads=[p4_t, cst_t], writes=[at_t])
            p5, p5_t = ps.next()
            s.op("pe", lambda e, p5=p5, at=at, vv=vv, c=c: e.matmul(p5[:], lhsT=at[:], rhs=vv[:, c, :],
                                                                     start=True, stop=False),
                 reads=[at_t, vv_t], writes=[p5_t])
            for dc in range(2):
                s.op("pe", lambda e, p5=p5, qd=qd, dc=dc: e.matmul(p5[:], lhsT=qd[:, dc, :], rhs=Sb[:, dc, :],
                                                                    start=False, stop=(dc == 1)),
                     reads=[qd_t, Sb_t], writes=[p5_t])
            ob, ob_t = osb.next()
            s.op("act", lambda e, ob=ob, p5=p5: e.copy(out=ob[:], in_=p5[:]), reads=[p5_t], writes=[ob_t])
            cx.store("sp", o_d[t0 + c * 128:t0 + (c + 1) * 128, :], ob[:], ob_t)
            for dc in range(2):
                p6, p6_t = ps.next()
                s.op("pe", lambda e, p6=p6, kdc=kdc, vv=vv, c=c, dc=dc: e.matmul(
                    p6[:], lhsT=kdc[:, dc * 128:(dc + 1) * 128], rhs=vv[:, c, :], start=True, stop=True),
                    reads=[kdc_t, vv_t], writes=[p6_t])
                s.op("dve", lambda e, p6=p6, e1=e1, dc=dc: e.scalar_tensor_tensor(
                    out=Sf[:, dc, :], in0=Sf[:, dc, :], scalar=e1[:, dc, 127:128], in1=p6[:],
                    op0=ALU.mult, op1=ALU.add), reads=[Sf_t, e1_t, p6_t], writes=[Sf_t])
            s.op("act", lambda e: e.copy(out=Sb[:], in_=Sf[:]), reads=[Sf_t], writes=[Sb_t])
    return cx.finish()


def gla_consts():
    idx = np.arange(128)
    tri = (idx[:, None] <= idx[None, :]).astype(np.float32)
    triS = tri * (-1.0 / 16.0)
    triUS = (1.0 - tri) * (-1.0 / 16.0)
    maskU = tri
    return np.ascontiguousarray(np.concatenate([triS, triUS, maskU], axis=1).astype(np.float32))


SC = 64
NSUB = 128 // SC
NLVL = 5
NEG = -30000.0
CB = 256


def gdn_consts():
    idx = np.arange(128)
    same = (idx[:, None] // SC) == (idx[None, :] // SC)
    ident = np.eye(128, dtype=np.float32)
    ones = np.ones((128, 128), np.float32)
    tri = ((idx[:, None] <= idx[None, :]) & same).astype(np.float32)
    nms = np.where((idx[None, :] > idx[:, None]) & same, 0.0, NEG).astype(np.float32)
    nmi = np.where((idx[None, :] >= idx[:, None]) & same, 0.0, NEG).astype(np.float32)
    blk = same.astype(np.float32)
    brow = [np.broadcast_to(((idx // SC) == b)[:, None], (128, 128)).astype(np.float32) for b in range(NSUB)]
    return np.ascontiguousarray(np.concatenate([ident, ones, tri, nms, nmi, blk] + brow, axis=1))


def build_gdn_scan(Tn=T):
    cx = Ctx()
    s = cx.s
    NCH = Tn // 128
    W = NCH * 8
    xT_d = cx.dram_in("xT", [2048, Tn])
    cw_d = cx.dram_in("cw", [128, 16 * CONV_W])
    ba_d = cx.dram_in("ba", [Tn, 16])
    hp_d = cx.dram_in("hp", [128, 2 * W])
    cst_d = cx.dram_in("cst", [128, (6 + NSUB) * 128])
    idb_d = cx.dram_in("ident", [128, 128], BF16)
    o_d = cx.dram_out("o", [Tn, 1024])

    cst, cst_t = cx.sb([128, (6 + NSUB) * 128], F32, "cst")
    cx.load("sp", cst[:], cst_d[:, :], cst_t)
    ident, ones, tri, nms, nmi, blk = [cst[:, i * 128:(i + 1) * 128] for i in range(6)]
    brow = [cst[:, (6 + b) * 128:(7 + b) * 128] for b in range(NSUB)]
    idb, idb_t = cx.sb([128, 128], BF16, "identb")
    cx.load("sp", idb[:], idb_d[:, :], idb_t)
    cw, cw_t = cx.sb([128, 16 * CONV_W], F32, "cw")
    cx.load("sp", cw[:], cw_d[:, :], cw_t)
    hp, hp_t = cx.sb([128, 2 * W], F32, "hp")
    cx.load("sp", hp[:], hp_d[:, :], hp_t)
    ba, ba_t = cx.sb([128, NCH, 16], F32, "ba")
    cx.load("sp", ba[:], ba_d[:, :].rearrange("(n p) c -> p n c", p=128), ba_t)
    epsc, epsc_t = cx.sb([128, 1], F32, "epsc")
    s.op("dve", lambda e: e.memset(epsc[:], EPS), writes=[epsc_t])

    psw = Rot([cx.ps([128, 512], F32, f"psw{i}") for i in range(2)])
    pss = Rot([cx.ps([128, 512], F32, f"pss{i}") for i in range(5)])
    pstb = Rot([cx.ps([128, 1024], BF16, "pstb")])

    def gt(name):
        return cx.sb([128, W], F32, name)
    g, g_t = gt("g")
    l2, l2_t = gt("l2")
    beta, beta_t = gt("beta")
    tmpw, tmpw_t = gt("tmpw")
    c_sb, c_t = gt("c_sb")
    negc, negc_t = gt("negc")
    expc, expc_t = gt("expc")
    clb, clb_t = gt("clb")
    bexpc, bexpc_t = gt("bexpc")
    kdsc, kdsc_t = gt("kdsc")
    cdb = [gt(f"cdb{b}") for b in range(NSUB)]
    bv3 = ba[:, :, 0:8]
    av3 = ba[:, :, 8:16]
    as3 = lambda t: t[:].rearrange("p (n h) -> p n h", h=8)
    s.op("dve", lambda e: e.tensor_tensor(out=as3(tmpw), in0=av3, in1=as3(hp)[:, NCH:2 * NCH, :], op=ALU.add),
         reads=[ba_t, hp_t], writes=[tmpw_t])
    s.op("act", lambda e: e.activation(out=tmpw[:], in_=tmpw[:], func=AF.Exp), reads=[tmpw_t], writes=[tmpw_t])
    s.op("act", lambda e: e.activation(out=tmpw[:], in_=tmpw[:], func=AF.Ln, bias=1.0), reads=[tmpw_t],
         writes=[tmpw_t])
    s.op("act", lambda e: e.activation(out=g[:], in_=hp[:, 0:W], func=AF.Exp), reads=[hp_t], writes=[g_t])
    s.op("dve", lambda e: e.scalar_tensor_tensor(out=g[:], in0=tmpw[:], scalar=-1.0, in1=g[:], op0=ALU.mult,
                                                 op1=ALU.mult), reads=[tmpw_t, g_t], writes=[g_t])
    s.op("act", lambda e: e.activation(out=as3(l2), in_=bv3, func=AF.Exp, scale=-1.0), reads=[ba_t], writes=[l2_t])
    s.op("act", lambda e: e.activation(out=l2[:], in_=l2[:], func=AF.Ln, bias=1.0), reads=[l2_t], writes=[l2_t])
    s.op("act", lambda e: e.activation(out=beta[:], in_=l2[:], func=AF.Exp, scale=-1.0), reads=[l2_t],
         writes=[beta_t])
    pc, pc_t = psw.next()
    s.op("pe", lambda e: e.matmul(pc[:, 0:W], lhsT=tri, rhs=g[:], start=True, stop=True), reads=[cst_t, g_t],
         writes=[pc_t])
    s.op("act", lambda e: e.copy(out=c_sb[:], in_=pc[:, 0:W]), reads=[pc_t], writes=[c_t])
    s.op("act", lambda e: e.activation(out=expc[:], in_=pc[:, 0:W], func=AF.Exp), reads=[pc_t], writes=[expc_t])
    s.op("dve", lambda e: e.tensor_scalar(out=negc[:], in0=pc[:, 0:W], scalar1=-1.0, scalar2=None, op0=ALU.mult),
         reads=[pc_t], writes=[negc_t])
    s.op("dve", lambda e: e.tensor_tensor(out=clb[:], in0=pc[:, 0:W], in1=l2[:], op=ALU.subtract),
         reads=[pc_t, l2_t], writes=[clb_t])
    s.op("act", lambda e: e.activation(out=bexpc[:], in_=clb[:], func=AF.Exp), reads=[clb_t], writes=[bexpc_t])
    pl, pl_t = psw.next()
    s.op("pe", lambda e: e.matmul(pl[:, 0:W], lhsT=blk, rhs=g[:], start=True, stop=True), reads=[cst_t, g_t],
         writes=[pl_t])
    s.op("dve", lambda e: e.tensor_tensor(out=kdsc[:], in0=pl[:, 0:W], in1=c_sb[:], op=ALU.subtract),
         reads=[pl_t, c_t], writes=[kdsc_t])
    s.op("act", lambda e: e.activation(out=kdsc[:], in_=kdsc[:], func=AF.Exp), reads=[kdsc_t], writes=[kdsc_t])
    for b in range(NSUB):
        pb, pb_t = psw.next()
        s.op("pe", lambda e, pb=pb, b=b: e.matmul(pb[:, 0:W], lhsT=brow[b], rhs=g[:], start=True, stop=True),
             reads=[cst_t, g_t], writes=[pb_t])
        s.op("act", lambda e, pb=pb, b=b: e.activation(out=cdb[b][0][:], in_=pb[:, 0:W], func=AF.Exp),
             reads=[pb_t], writes=[cdb[b][1]])

    xin = Rot([cx.sb([128, 16, CB + 4], F32, f"xin{i}") for i in range(2)])
    acc = Rot([cx.sb([128, CB], F32, f"acc{i}") for i in range(4)])
    acc2 = Rot([cx.sb([128, CB], F32, f"acc2{i}") for i in range(2)])
    sl = Rot([cx.sb([128, CB], F32, f"sl{i}") for i in range(3)])
    sq = Rot([cx.sb([128, CB], F32, f"sq{i}") for i in range(2)])
    rinv = Rot([cx.sb([128, CB], F32, f"rinv{i}") for i in range(2)])
    qkT = Rot([cx.sb([128, 8, CB], BF16, f"qkT{i}") for i in range(2)])
    vT = Rot([cx.sb([128, 8, CB], BF16, f"vT{i}") for i in range(2)])
    G = Rot([cx.sb([128, 256], F32, f"G{i}") for i in range(8)])

    def gb(nm, dt):
        return [cx.sb([128, 4, 128], dt, f"{nm}{g_}") for g_ in range(2)]
    Sf, ub, tmpo = gb("Sf", F32), gb("u", F32), gb("tmpo", F32)
    Sb, TTb, bvb, kbd, kdc, attn, wTb, vnew = (gb(n_, BF16) for n_ in
                                               ("Sb", "TT", "bv", "kbd", "kdc", "attn", "wT", "vn"))
    for g_ in range(2):
        s.op("dve", lambda e, g_=g_: e.memset(Sf[g_][0][:], 0.0), writes=[Sf[g_][1]])
        s.op("pool", lambda e, g_=g_: e.memset(Sb[g_][0][:], 0.0), writes=[Sb[g_][1]])
        s.op("pool", lambda e, g_=g_: e.memset(vnew[g_][0][:], 0.0), writes=[vnew[g_][1]])
    f4 = Rot([cx.sb([128, 4, 128], F32, f"f4_{i}") for i in range(12)])
    E1 = Rot([cx.sb([128, 4, 128], F32, f"E1_{i}") for i in range(2)])
    E2 = Rot([cx.sb([128, 4, 128], F32, f"E2_{i}") for i in range(2)])
    dg = Rot([cx.sb([128, 4, 128], F32, f"dg{i}") for i in range(4)])
    osb = Rot([cx.sb([128, 1024], F32, f"osb{i}") for i in range(2)])
    v4 = lambda ap: ap.rearrange("p (h t) -> p h t", t=128)

    def group_steps(gi, n, cs, qk, qk_t, vt, vt_t, Gs, ob, ob_t):
        hs = range(4)
        wcol = lambda t, h: t[:, n * 8 + gi * 4 + h:n * 8 + gi * 4 + h + 1]
        pt, pt_t = pstb.next()
        for h in hs:
            s.op("pe", lambda e, h=h: e.transpose(out=pt[:, h * 128:(h + 1) * 128], in_=vt[:, gi * 4 + h, cs],
                                                  identity=idb[:]), reads=[vt_t, idb_t], writes=[pt_t])
        for q in range(2):
            s.op("pe", lambda e, q=q: e.transpose(out=pt[:, (4 + q) * 128:(5 + q) * 128],
                                                  in_=qk[:, 4 + gi * 2 + q, cs], identity=idb[:]),
                 reads=[qk_t, idb_t], writes=[pt_t])
        for h in hs:
            kk = pt[:, (4 + h // 2) * 128:(5 + h // 2) * 128]
            s.op("dve", lambda e, h=h: e.tensor_scalar(out=bvb[gi][0][:, h, :], in0=pt[:, h * 128:(h + 1) * 128],
                                                       scalar1=wcol(beta, h), scalar2=None, op0=ALU.mult),
                 reads=[pt_t, beta_t], writes=[bvb[gi][1]])
            s.op("dve", lambda e, h=h, kk=kk: e.tensor_scalar(out=kbd[gi][0][:, h, :], in0=kk, scalar1=wcol(bexpc, h),
                                                              scalar2=None, op0=ALU.mult),
                 reads=[pt_t, bexpc_t], writes=[kbd[gi][1]])
            s.op("dve", lambda e, h=h, kk=kk: e.tensor_scalar(out=kdc[gi][0][:, h, :], in0=kk, scalar1=wcol(kdsc, h),
                                                              scalar2=None, op0=ALU.mult),
                 reads=[pt_t, kdsc_t], writes=[kdc[gi][1]])
        yield
        Es = []
        for (src, src_t, nm, Epool) in ((clb, clb_t, nms, E1), (c_sb, c_t, nmi, E2)):
            d1, d1_t = dg.next()
            for h in hs:
                s.op("pool", lambda e, h=h, d1=d1, src=src: e.tensor_scalar(
                    out=d1[:, h, :], in0=ident, scalar1=wcol(src, h), scalar2=None, op0=ALU.mult),
                    reads=[cst_t, src_t], writes=[d1_t])
            pe_, pe_t = pss.next()
            for h in hs:
                s.op("pe", lambda e, h=h, pe_=pe_, d1=d1: e.matmul(pe_[:, h * 128:(h + 1) * 128], lhsT=ones,
                                                                   rhs=d1[:, h, :], start=True, stop=False),
                     reads=[cst_t, d1_t], writes=[pe_t])
                s.op("pe", lambda e, h=h, pe_=pe_, nm=nm: e.matmul(pe_[:, h * 128:(h + 1) * 128], lhsT=ident, rhs=nm,
                                                                   start=False, stop=True),
                     reads=[cst_t], writes=[pe_t])
            Et, Et_t = Epool.next()
            for h in hs:
                s.op("act", lambda e, h=h, Et=Et, pe_=pe_: e.activation(
                    out=Et[:, h, :], in_=pe_[:, h * 128:(h + 1) * 128], func=AF.Exp, bias=wcol(negc, h)),
                    reads=[pe_t, negc_t], writes=[Et_t])
            Es.append((Et, Et_t))
            yield
        (E1t, E1t_t), (E2t, E2t_t) = Es
        P, P_t = f4.next()
        R, R_t = f4.next()
        for h in hs:
            Gq, Gq_t = Gs[gi * 2 + h // 2]
            s.op("pool", lambda e, h=h, Gq=Gq: e.tensor_tensor(out=attn[gi][0][:, h, :], in0=Gq[:, 128:256],
                                                               in1=E2t[:, h, :], op=ALU.mult),
                 reads=[Gq_t, E2t_t], writes=[attn[gi][1]])
            s.op("dve", lambda e, h=h, Gq=Gq, P=P: e.scalar_tensor_tensor(
                out=P[:, h, :], in0=Gq[:, 0:128], scalar=-1.0, in1=E1t[:, h, :], op0=ALU.mult, op1=ALU.mult),
                reads=[Gq_t, E1t_t], writes=[P_t])
        for h in hs:
            s.op("pool", lambda e, h=h, P=P, R=R: e.tensor_tensor(out=R[:, h, :], in0=P[:, h, :], in1=ident,
                                                                  op=ALU.add),
                 reads=[P_t, cst_t], writes=[R_t])
        pp, pp_t = pss.next()
        for h in hs:
            s.op("pe", lambda e, h=h, P=P, pp=pp: e.matmul(pp[:, h * 128:(h + 1) * 128], lhsT=P[:, h, :], rhs=ident,
                                                           start=True, stop=True),
                 reads=[P_t, cst_t], writes=[pp_t])
        PT, PT_t = f4.next()
        s.op("act", lambda e, PT=PT, pp=pp: e.copy(out=PT[:], in_=v4(pp[:])), reads=[pp_t], writes=[PT_t])
        yield
        for lvl in range(1, NLVL + 1):
            p1, p1_t = pss.next()
            for h in hs:
                s.op("pe", lambda e, h=h, p1=p1, P=P, PT=PT: e.matmul(p1[:, h * 128:(h + 1) * 128], lhsT=P[:, h, :],
                                                                      rhs=PT[:, h, :], start=True, stop=True),
                     reads=[P_t, PT_t], writes=[p1_t])
            PTn, PTn_t = f4.next()
            s.op("act", lambda e, PTn=PTn, p1=p1: e.copy(out=PTn[:], in_=v4(p1[:])), reads=[p1_t], writes=[PTn_t])
            if lvl < NLVL:
                p2, p2_t = pss.next()
                for h in hs:
                    s.op("pe", lambda e, h=h, p2=p2, P=P, PT=PT: e.matmul(p2[:, h * 128:(h + 1) * 128],
                                                                          lhsT=PT[:, h, :], rhs=P[:, h, :],
                                                                          start=True, stop=True),
                         reads=[P_t, PT_t], writes=[p2_t])
                Pn, Pn_t = f4.next()
                s.op("dve", lambda e, Pn=Pn, p2=p2: e.tensor_copy(out=Pn[:], in_=v4(p2[:])), reads=[p2_t],
                     writes=[Pn_t])
            p3, p3_t = pss.next()
            for h in hs:
                s.op("pe", lambda e, h=h, p3=p3, PTn=PTn, R=R: e.matmul(p3[:, h * 128:(h + 1) * 128],
                                                                        lhsT=PTn[:, h, :], rhs=R[:, h, :],
                                                                        start=True, stop=True),
                     reads=[PTn_t, R_t], writes=[p3_t])
            Rn, Rn_t = f4.next() if lvl < NLVL else TTb[gi]
            s.op("dve", lambda e, Rn=Rn, p3=p3, R=R: e.tensor_tensor(out=Rn[:], in0=v4(p3[:]), in1=R[:], op=ALU.add),
                 reads=[p3_t, R_t], writes=[Rn_t])
            R, R_t = Rn, Rn_t
            if lvl < NLVL:
                P, P_t, PT, PT_t = Pn, Pn_t, PTn, PTn_t
            yield
        pu, pu_t = pss.next()
        for h in hs:
            s.op("pe", lambda e, h=h, pu=pu: e.matmul(pu[:, h * 128:(h + 1) * 128], lhsT=TTb[gi][0][:, h, :],
                                                      rhs=bvb[gi][0][:, h, :], start=True, stop=True),
                 reads=[TTb[gi][1], bvb[gi][1]], writes=[pu_t])
        s.op("act", lambda e, pu=pu: e.copy(out=ub[gi][0][:], in_=v4(pu[:])), reads=[pu_t], writes=[ub[gi][1]])
        pw_, pw_t = pss.next()
        for h in hs:
            s.op("pe", lambda e, h=h, pw_=pw_: e.matmul(pw_[:, h * 128:(h + 1) * 128], lhsT=kbd[gi][0][:, h, :],
                                                        rhs=TTb[gi][0][:, h, :], start=True, stop=True),
                 reads=[TTb[gi][1], kbd[gi][1]], writes=[pw_t])
        s.op("act", lambda e, pw_=pw_: e.copy(out=wTb[gi][0][:], in_=v4(pw_[:])), reads=[pw_t], writes=[wTb[gi][1]])
        yield
        for b in range(NSUB):
            rb = slice(b * SC, (b + 1) * SC)
            P1, P1_t = pss.next()
            for h in hs:
                s.op("pe", lambda e, h=h, P1=P1: e.matmul(P1[:, h * 128:(h + 1) * 128], lhsT=wTb[gi][0][:, h, :],
                                                          rhs=Sb[gi][0][:, h, :], start=True, stop=True),
                     reads=[wTb[gi][1], Sb[gi][1]], writes=[P1_t])
            s.op("dve", lambda e, P1=P1, rb=rb: e.tensor_tensor(out=vnew[gi][0][rb, :, :], in0=ub[gi][0][rb, :, :],
                                                                in1=v4(P1[rb, :]), op=ALU.subtract),
                 reads=[ub[gi][1], P1_t], writes=[vnew[gi][1]])
            P3, P3_t = pss.next()
            for h in hs:
                s.op("pe", lambda e, h=h, P3=P3: e.matmul(P3[:, h * 128:(h + 1) * 128],
                                                          lhsT=qk[:, gi * 2 + h // 2, cs], rhs=Sb[gi][0][:, h, :],
                                                          start=True, stop=True),
                     reads=[qk_t, Sb[gi][1]], writes=[P3_t])
            yield
            P2, P2_t = pss.next()
            for h in hs:
                s.op("pe", lambda e, h=h, P2=P2, rb=rb: e.matmul(P2[:, h * 128:(h + 1) * 128],
                                                                 lhsT=attn[gi][0][rb, h, :], rhs=vnew[gi][0][rb, h, :],
                                                                 start=True, stop=True),
                     reads=[attn[gi][1], vnew[gi][1]], writes=[P2_t])
            s.op("act", lambda e, P2=P2, rb=rb: e.copy(out=tmpo[gi][0][rb, :, :], in_=v4(P2[rb, :])), reads=[P2_t],
                 writes=[tmpo[gi][1]])
            for h in hs:
                s.op("dve", lambda e, h=h, P3=P3, rb=rb: e.scalar_tensor_tensor(
                    out=ob[rb, (gi * 4 + h) * 128:(gi * 4 + h + 1) * 128], in0=P3[rb, h * 128:(h + 1) * 128],
                    scalar=wcol(expc, h)[rb, :], in1=tmpo[gi][0][rb, h, :], op0=ALU.mult, op1=ALU.add),
                    reads=[P3_t, expc_t, tmpo[gi][1]], writes=[ob_t])
            P4, P4_t = pss.next()
            for h in hs:
                s.op("pe", lambda e, h=h, P4=P4, rb=rb: e.matmul(P4[:, h * 128:(h + 1) * 128],
                                                                 lhsT=kdc[gi][0][rb, h, :], rhs=vnew[gi][0][rb, h, :],
                                                                 start=True, stop=True),
                     reads=[kdc[gi][1], vnew[gi][1]], writes=[P4_t])
            for h in hs:
                s.op("dve", lambda e, h=h, P4=P4, b=b: e.scalar_tensor_tensor(
                    out=Sf[gi][0][:, h, :], in0=Sf[gi][0][:, h, :], scalar=wcol(cdb[b][0], h),
                    in1=P4[:, h * 128:(h + 1) * 128], op0=ALU.mult, op1=ALU.add),
                    reads=[Sf[gi][1], cdb[b][1], P4_t], writes=[Sf[gi][1]])
            s.op("act", lambda e: e.copy(out=Sb[gi][0][:], in_=Sf[gi][0][:]), reads=[Sf[gi][1]], writes=[Sb[gi][1]])
            yield

    nblk = Tn // CB
    for bi in range(nblk):
        t0 = bi * CB
        xi, xi_t = xin.next()
        lo = 2 if bi == 0 else 0
        hi = CB + 2 if bi == nblk - 1 else CB + 4
        if bi == 0:
            s.op("pool", lambda e, xi=xi: e.memset(xi[:, :, 0:2], 0.0), writes=[xi_t])
        if bi == nblk - 1:
            s.op("pool", lambda e, xi=xi: e.memset(xi[:, :, CB + 2:CB + 4], 0.0), writes=[xi_t])

        def ldfn(e, sem, xi=xi, lo=lo, hi=hi, t0=t0):
            for half in range(2):
                src = xT_d[half * 1024:(half + 1) * 1024, t0 - 2 + lo:t0 - 2 + hi].rearrange("(j p) t -> p j t", p=128)
                e.dma_start(out=xi[:, half * 8:(half + 1) * 8, lo:hi], in_=src).then_inc(sem, 16)
        s.dma("sp", ldfn, xi_t, "ld", writes=[xi_t], n=2)
        qk, qk_t = qkT.next()
        vt, vt_t = vT.next()
        for j in range(16):
            eng = "dve" if j % 2 == 0 else "pool"
            a, a_t = acc.next()
            s.op(eng, lambda e, a=a, xi=xi, j=j: e.tensor_scalar(
                out=a[:], in0=xi[:, j, 0:CB], scalar1=cw[:, j * CONV_W:j * CONV_W + 1], scalar2=None, op0=ALU.mult),
                reads=[xi_t, cw_t], writes=[a_t])
            for k in range(1, CONV_W):
                if eng == "dve":
                    s.op(eng, lambda e, a=a, xi=xi, j=j, k=k: e.scalar_tensor_tensor(
                        out=a[:], in0=xi[:, j, k:k + CB], scalar=cw[:, j * CONV_W + k:j * CONV_W + k + 1], in1=a[:],
                        op0=ALU.mult, op1=ALU.add), reads=[xi_t, cw_t, a_t], writes=[a_t])
                else:
                    a2, a2_t = acc2.next()
                    s.op(eng, lambda e, a2=a2, xi=xi, j=j, k=k: e.tensor_scalar(
                        out=a2[:], in0=xi[:, j, k:k + CB], scalar1=cw[:, j * CONV_W + k:j * CONV_W + k + 1],
                        scalar2=None, op0=ALU.mult), reads=[xi_t, cw_t], writes=[a2_t])
                    s.op(eng, lambda e, a=a, a2=a2: e.tensor_tensor(out=a[:], in0=a[:], in1=a2[:], op=ALU.add),
                         reads=[a_t, a2_t], writes=[a_t])
            if j >= 8:
                s.op("act", lambda e, a=a, vt=vt, j=j: e.activation(out=vt[:, j - 8, :], in_=a[:], func=AF.Silu),
                     reads=[a_t], writes=[vt_t])
                continue
            sl1, sl1_t = sl.next()
            s.op("act", lambda e, a=a, sl1=sl1: e.activation(out=sl1[:], in_=a[:], func=AF.Silu), reads=[a_t],
                 writes=[sl1_t])
            sq1, sq1_t = sq.next()
            s.op("pool", lambda e, sq1=sq1, sl1=sl1: e.tensor_tensor(out=sq1[:], in0=sl1[:], in1=sl1[:], op=ALU.mult),
                 reads=[sl1_t], writes=[sq1_t])
            pw, pw_t = psw.next()
            s.op("pe", lambda e, pw=pw, sq1=sq1: e.matmul(pw[:, 0:CB], lhsT=ones, rhs=sq1[:], start=True, stop=True),
                 reads=[cst_t, sq1_t], writes=[pw_t])
            ri, ri_t = rinv.next()
            s.op("act", lambda e, ri=ri, pw=pw: e.activation(out=ri[:], in_=pw[:, 0:CB], func=AF.Ln, bias=epsc[:, 0:1]),
                 reads=[pw_t, epsc_t], writes=[ri_t])
            s.op("act", lambda e, ri=ri: e.activation(out=ri[:], in_=ri[:], func=AF.Exp, scale=-0.5), reads=[ri_t],
                 writes=[ri_t])
            sc_ = float(GDN_DK) ** -0.5 if j < 4 else 1.0
            s.op("dve", lambda e, qk=qk, j=j, sl1=sl1, ri=ri, sc_=sc_: e.scalar_tensor_tensor(
                out=qk[:, j, :], in0=sl1[:], scalar=sc_, in1=ri[:], op0=ALU.mult, op1=ALU.mult),
                reads=[sl1_t, ri_t], writes=[qk_t])

        for c in range(CB // 128):
            n = bi * (CB // 128) + c
            cs = slice(c * 128, (c + 1) * 128)
            ob, ob_t = osb.next()
            Gs = []
            for qh in range(4):
                pg, pg_t = psw.next()
                s.op("pe", lambda e, pg=pg, qh=qh, qk=qk, cs=cs: e.matmul(
                    pg[:, 0:128], lhsT=qk[:, 4 + qh, cs], rhs=qk[:, 4 + qh, cs], start=True, stop=True),
                    reads=[qk_t], writes=[pg_t])
                s.op("pe", lambda e, pg=pg, qh=qh, qk=qk, cs=cs: e.matmul(
                    pg[:, 128:256], lhsT=qk[:, 4 + qh, cs], rhs=qk[:, qh, cs], start=True, stop=True),
                    reads=[qk_t], writes=[pg_t])
                Gq, Gq_t = G.next()
                s.op("act", lambda e, Gq=Gq, pg=pg: e.copy(out=Gq[:], in_=pg[:, 0:256]), reads=[pg_t], writes=[Gq_t])
                Gs.append((Gq, Gq_t))
            gens = [group_steps(gi, n, cs, qk, qk_t, vt, vt_t, Gs, ob, ob_t) for gi in range(2)]
            live = list(gens)
            while live:
                for gen in list(live):
                    try:
                        next(gen)
                    except StopIteration:
                        live.remove(gen)
            cx.store("sp", o_d[n * 128:(n + 1) * 128, :], ob[:], ob_t)
    return cx.finish()


def dense_main(cx, dn, d, ncol_o, hw, nxt_cols):
    s = cx.s
    pc = Rot([cx.sb([128, 3, 512], F32, f"pc{i}") for i in range(2)])
    sgt = Rot([cx.sb([128, 512], F32, f"sg{i}") for i in range(2)])
    act, act_t = cx.sb([128, 4, TL], BF16, "actblk")
    pT, pT_t = cx.sb([128, 2, TL], BF16, "pT")
    onw, onw_t = cx.sb([128, 512], F32, "onw")
    cx.load("sp", onw[:], d["onw"][:, :], onw_t)

    def gain(name):
        g, g_t = dn.gains.next()
        cx.load("sp", g[:], d[name][:, :], g_t)
        return g, g_t

    def add_into_h(i, c, w, ps, ps_t):
        s.op("dve", lambda e: e.tensor_tensor(out=dn.h[:, i, c:c + w], in0=dn.h[:, i, c:c + w], in1=ps[:, 0:w],
                                              op=ALU.add), reads=[dn.h_t[i], ps_t], writes=[dn.h_t[i]])

    dn.load_h(d["x"])
    for kh in range(ncol_o // D):
        for i in range(NT):
            xn, xn_t = dn.xn.next()
            for pcs in range(D // 512):
                c0 = kh * D + pcs * 512
                t3, t3_t = pc.next()
                rows = slice(i * 128, (i + 1) * 128)

                def ld(e, sem, t3=t3, rows=rows, c0=c0):
                    for j, nm in enumerate(("of", "ob", "og")):
                        e.dma_start(out=t3[:, j, :], in_=d[nm][rows, c0:c0 + 512]).then_inc(sem, 16)
                s.dma("sp", ld, t3_t, "ld", writes=[t3_t], n=3)
                s.op("pool", lambda e, t3=t3: e.tensor_tensor(out=t3[:, 0, :], in0=t3[:, 0, :], in1=t3[:, 1, :],
                                                              op=ALU.add), reads=[t3_t], writes=[t3_t])
                s.op("act", lambda e, t3=t3: e.activation(out=t3[:, 2, :], in_=t3[:, 2, :], func=AF.Silu),
                     reads=[t3_t], writes=[t3_t])
                s.op("pool", lambda e, t3=t3: e.tensor_tensor(out=t3[:, 2, :], in0=t3[:, 2, :], in1=onw[:],
                                                              op=ALU.mult), reads=[t3_t, onw_t], writes=[t3_t])
                for sub in range(512 // hw):
                    cc = slice(sub * hw, (sub + 1) * hw)
                    r, r_t = dn.rstd(t3[:, 0, cc], [t3_t], hw)
                    s.op("dve", lambda e, t3=t3, cc=cc, r=r, xn=xn, pcs=pcs, sub=sub: e.scalar_tensor_tensor(
                        out=xn[:, pcs * 512 + sub * hw:pcs * 512 + (sub + 1) * hw], in0=t3[:, 0, cc], scalar=r,
                        in1=t3[:, 2, cc], op0=ALU.mult, op1=ALU.mult), reads=[t3_t, r_t], writes=[xn_t])
            dn.to_feat(xn, xn_t, i, D)
        dn.mm_tok(d["w_out"], KD, 0, D, add_into_h, r0=kh * D)

    g, g_t = gain("ffn_norm")
    dn.norm_to_feat(g, g_t)
    for fb in range(D_FF // 512):
        wg, wg_t = dn.load_w(d["ffn_w_in"], 0, KD, fb * 512, 512)
        wu, wu_t = dn.load_w(d["ffn_w_in"], 0, KD, D_FF + fb * 512, 512)
        wo, wo_t = dn.load_w(d["ffn_w_out"], fb * 512, 4, 0, D)
        for j in range(4):
            for th in range(TL // 512):
                ts_ = slice(th * 512, (th + 1) * 512)
                rt = [dn.aT_t[q] for q in range(th * 4, th * 4 + 4)]
                pg, pg_t = dn.psm.next()
                pu, pu_t = dn.psm.next()
                for (p_, p_t, w_, w_t) in ((pg, pg_t, wg, wg_t), (pu, pu_t, wu, wu_t)):
                    for k in range(KD):
                        s.op("pe", lambda e, p_=p_, w_=w_, k=k, j=j, ts_=ts_: e.matmul(
                            p_[:], lhsT=w_[:, k, j * 128:(j + 1) * 128], rhs=dn.aT[:, k, ts_], start=(k == 0),
                            stop=(k == KD - 1)), reads=rt + [w_t], writes=[p_t])
                sg, sg_t = sgt.next()
                s.op("act", lambda e, sg=sg, pg=pg: e.activation(out=sg[:], in_=pg[:], func=AF.Silu), reads=[pg_t],
                     writes=[sg_t])
                s.op("dve", lambda e, sg=sg, pu=pu, j=j, ts_=ts_: e.tensor_tensor(out=act[:, j, ts_], in0=sg[:],
                                                                                 in1=pu[:], op=ALU.mult),
                     reads=[sg_t, pu_t], writes=[act_t])
        for i in range(NT):
            for dt_ in range(D // 512):
                ps, ps_t = dn.psm.next()
                for j in range(4):
                    s.op("pe", lambda e, ps=ps, j=j, i=i, dt_=dt_, wo=wo: e.matmul(
                        ps[:], lhsT=act[:, j, i * 128:(i + 1) * 128], rhs=wo[:, j, dt_ * 512:(dt_ + 1) * 512],
                        start=(j == 0), stop=(j == 3)), reads=[act_t, wo_t], writes=[ps_t])
                add_into_h(i, dt_ * 512, 512, ps, ps_t)

    g, g_t = gain("ple_norm")
    dn.norm_to_feat(g, g_t)
    for i in range(NT):
        t3, t3_t = pc.next()
        cx.load("sp", t3[:, 0, 0:PLE], d["p"][i * 128:(i + 1) * 128, :], t3_t)
        xn, xn_t = dn.xn.next()
        s.op("act", lambda e, t3=t3, xn=xn: e.copy(out=xn[:, 0:PLE], in_=t3[:, 0, 0:PLE]), reads=[t3_t], writes=[xn_t])
        pt, pt_t = dn.pst.next()
        for j in range(2):
            s.op("pe", lambda e, j=j, pt=pt, xn=xn: e.transpose(out=pt[:, j * 128:(j + 1) * 128],
                                                                 in_=xn[:, j * 128:(j + 1) * 128],
                                                                 identity=dn.ident[:]),
                 reads=[xn_t, dn.ident_t], writes=[pt_t])
        s.op("act", lambda e, i=i, pt=pt: e.copy(out=pT[:, :, i * 128:(i + 1) * 128],
                                                 in_=pt[:, 0:256].rearrange("p (k t) -> p k t", t=128)),
             reads=[pt_t], writes=[pT_t])
    for c in range(0, D, 512):
        wgt, wgt_t = dn.load_w(d["ple_w_gate"], 0, KD, c, 512)
        wpj, wpj_t = dn.load_w(d["ple_w_proj"], 0, 2, c, 512)
        for i in range(NT):
            pg, pg_t = dn.psm.next()
            for k in range(KD):
                s.op("pe", lambda e, pg=pg, k=k, i=i, wgt=wgt: e.matmul(
                    pg[:], lhsT=dn.aT[:, k, i * 128:(i + 1) * 128], rhs=wgt[:, k, :], start=(k == 0),
                    stop=(k == KD - 1)), reads=[dn.aT_t[i], wgt_t], writes=[pg_t])
            pp, pp_t = dn.psm.next()
            for k in range(2):
                s.op("pe", lambda e, pp=pp, k=k, i=i, wpj=wpj: e.matmul(
                    pp[:], lhsT=pT[:, k, i * 128:(i + 1) * 128], rhs=wpj[:, k, :], start=(k == 0), stop=(k == 1)),
                    reads=[pT_t, wpj_t], writes=[pp_t])
            sg, sg_t = sgt.next()
            s.op("act", lambda e, sg=sg, pg=pg: e.activation(out=sg[:], in_=pg[:], func=AF.Sigmoid), reads=[pg_t],
                 writes=[sg_t])
            s.op("dve", lambda e, sg=sg, pp=pp: e.tensor_tensor(out=sg[:], in0=sg[:], in1=pp[:], op=ALU.mult),
                 reads=[sg_t, pp_t], writes=[sg_t])
            s.op("pool", lambda e, sg=sg, i=i, c=c: e.tensor_tensor(out=dn.h[:, i, c:c + 512], in0=dn.h[:, i, c:c + 512],
                                                                     in1=sg[:], op=ALU.add),
                 reads=[sg_t, dn.h_t[i]], writes=[dn.h_t[i]])

    if nxt_cols:
        for i in range(NT):
            cx.store("sp", d["h_out"][i * 128:(i + 1) * 128, :], dn.h[:, i, :], dn.h_t[i])
        g, g_t = gain("next_norm")
        dn.norm_to_feat(g, g_t)
        dn.proj_store(d["next_w"], nxt_cols, d["proj_out"])
    else:
        g, g_t = gain("next_norm")
        for i in range(NT):
            r, r_t = dn.rstd(dn.h[:, i, :], [dn.h_t[i]], D)
            s.op("dve", lambda e, i=i, r=r: e.scalar_tensor_tensor(
                out=dn.h[:, i, :], in0=dn.h[:, i, :], scalar=r, in1=g[:], op0=ALU.mult, op1=ALU.mult),
                reads=[dn.h_t[i], r_t, g_t], writes=[dn.h_t[i]])
            cx.store("sp", d["h_out"][i * 128:(i + 1) * 128, :], dn.h[:, i, :], dn.h_t[i])


def build_phase_main(ncol_o, hw, nxt_cols):
    cx = Ctx()
    d = {}
    d["x"] = cx.dram_in("x", [TL, D])
    for nm in ("of", "ob", "og"):
        d[nm] = cx.dram_in(nm, [TL, ncol_o])
    d["onw"] = cx.dram_in("onw", [128, 512])
    d["w_out"] = cx.dram_in("w_out", [ncol_o, D])
    d["ffn_norm"] = cx.dram_in("ffn_norm", [128, D])
    d["ffn_w_in"] = cx.dram_in("ffn_w_in", [D, 2 * D_FF])
    d["ffn_w_out"] = cx.dram_in("ffn_w_out", [D_FF, D])
    d["ple_norm"] = cx.dram_in("ple_norm", [128, D])
    d["ple_w_gate"] = cx.dram_in("ple_w_gate", [D, D])
    d["ple_w_proj"] = cx.dram_in("ple_w_proj", [PLE, D])
    d["p"] = cx.dram_in("p", [TL, PLE])
    d["next_norm"] = cx.dram_in("next_norm", [128, D])
    if nxt_cols:
        d["next_w"] = cx.dram_in("next_w", [D, nxt_cols])
        d["proj_out"] = cx.dram_out("proj_out", [TL, nxt_cols])
    id_d = cx.dram_in("ident", [128, 128], BF16)
    d["h_out"] = cx.dram_out("h_out", [TL, D])
    dn = Dense(cx, id_d)
    dense_main(cx, dn, d, ncol_o, hw, nxt_cols)
    return cx.finish()


_CACHE = {}


def _prog(key, fn):
    if key not in _CACHE:
        _CACHE[key] = fn()
    return _CACHE[key]


def _run(nc, in_maps):
    return run_bass_kernel_spmd(nc, in_maps, core_ids=list(range(NCORES))).results


def _c(a):
    return np.ascontiguousarray(a, dtype=np.float32)


def kernel(x, p, mixer_norm, gla_w_in, gla_w_gate_up, gla_b_gate, gla_out_norm, gla_w_out,
           gdn_w_in, gdn_conv, gdn_a_log, gdn_dt_bias, gdn_out_norm, gdn_w_out,
           ffn_norm, ffn_w_in, ffn_w_out, ple_norm, ple_w_gate, ple_w_proj, final_norm):
    f = lambda a: np.asarray(a, dtype=np.float32)
    x2 = f(x)[0]
    p = f(p)
    ident = bf16_ident()
    tok = lambda a, c: _c(a[c * TL:(c + 1) * TL])

    nc = _prog("A", lambda: build_phase_a(GLA_PROJ))
    g0 = rep128(f(mixer_norm)[0])
    w0 = _c(f(gla_w_in)[0])
    res = _run(nc, [{"x": tok(x2, c), "g": g0, "w": w0, "ident": ident} for c in range(NCORES)])
    proj0 = np.concatenate([r["proj"] for r in res], axis=0)

    q0, k0, v0 = proj0[:, 0:GLA_KD], proj0[:, GLA_KD:2 * GLA_KD], proj0[:, 2 * GLA_KD:2 * GLA_KD + GLA_VD]
    og0 = proj0[:, 2 * GLA_KD + GLA_VD:2 * GLA_KD + 2 * GLA_VD]
    lr0 = proj0[:, 2 * GLA_KD + 2 * GLA_VD:].reshape(T, 2, GLA_R)
    nc = _prog("B", lambda: build_gla_scan(T))
    cst = gla_consts()
    maps = []
    for c in range(NCORES):
        hh, z = c // 2, c % 2
        o_ = (lambda a: a[::-1]) if z else (lambda a: a)
        qh = o_(q0[:, hh * GLA_DK:(hh + 1) * GLA_DK])
        kh = o_(k0[:, hh * GLA_DK:(hh + 1) * GLA_DK])
        vh = o_(v0[:, hh * GLA_DV:(hh + 1) * GLA_DV])
        lrz = o_(lr0[:, z, :])
        maps.append({"qT": _c(qh.T), "kT": _c(kh.T), "k": _c(kh), "v": _c(vh), "lrT": _c(lrz.T),
                     "wgu": _c(f(gla_w_gate_up)[0, z][:, hh * GLA_DK:(hh + 1) * GLA_DK]),
                     "b_bc": rep128(f(gla_b_gate)[0, z, hh * GLA_DK:(hh + 1) * GLA_DK]), "cst": cst})
    res = _run(nc, maps)
    of0 = np.empty((T, GLA_VD), np.float32)
    ob0 = np.empty((T, GLA_VD), np.float32)
    for c in range(NCORES):
        hh, z = c // 2, c % 2
        if z == 0:
            of0[:, hh * GLA_DV:(hh + 1) * GLA_DV] = res[c]["o"]
        else:
            ob0[:, hh * GLA_DV:(hh + 1) * GLA_DV] = res[c]["o"][::-1]

    nc = _prog("C", lambda: build_phase_main(GLA_VD, GLA_DV, GDN_PROJ))
    shared = {"onw": rep128(f(gla_out_norm)[0]), "w_out": _c(f(gla_w_out)[0]),
              "ffn_norm": rep128(f(ffn_norm)[0]), "ffn_w_in": _c(f(ffn_w_in)[0]), "ffn_w_out": _c(f(ffn_w_out)[0]),
              "ple_norm": rep128(f(ple_norm)[0]), "ple_w_gate": _c(f(ple_w_gate)[0]),
              "ple_w_proj": _c(f(ple_w_proj)[0]), "next_norm": rep128(f(mixer_norm)[1]),
              "next_w": _c(f(gdn_w_in)[0]), "ident": ident}
    maps = []
    for c in range(NCORES):
        m = dict(shared)
        m.update({"x": tok(x2, c), "of": tok(of0, c), "ob": tok(ob0, c), "og": tok(og0, c), "p": tok(p[0, 0], c)})
        maps.append(m)
    res = _run(nc, maps)
    h1 = np.concatenate([r["h_out"] for r in res], axis=0)
    proj1 = np.concatenate([r["proj_out"] for r in res], axis=0)
    del proj0, of0, ob0

    z1 = proj1[:, GDN_CONV:GDN_CONV + GDN_VD]
    ba1 = proj1[:, GDN_CONV + GDN_VD:].reshape(T, 2, 2, GDN_VH)
    nc = _prog("D", lambda: build_gdn_scan(T))
    cstd = gdn_consts()
    conv = f(gdn_conv)[0]
    NCH = T // 128
    maps = []
    for c in range(NCORES):
        gq, z = c // 2, c % 2
        cols = np.concatenate([np.arange((4 * gq) * 128, (4 * gq + 4) * 128),
                               GDN_KD + np.arange((4 * gq) * 128, (4 * gq + 4) * 128),
                               2 * GDN_KD + np.arange((8 * gq) * 128, (8 * gq + 8) * 128)])
        xs = proj1[:, cols]
        cwsel = conv[:, cols]
        hsl = slice(8 * gq, 8 * gq + 8)
        ba = np.concatenate([ba1[:, 0, z, hsl], ba1[:, 1, z, hsl]], axis=1)
        if z:
            xs, ba, cwsel = xs[::-1], ba[::-1], cwsel[::-1]
        cw = _c(cwsel.T.reshape(16, 128, CONV_W).transpose(1, 0, 2).reshape(128, 16 * CONV_W))
        hp = np.concatenate([np.tile(f(gdn_a_log)[0, z, hsl], NCH), np.tile(f(gdn_dt_bias)[0, z, hsl], NCH)])
        maps.append({"xT": _c(xs.T), "cw": cw, "ba": _c(ba), "hp": rep128(hp), "cst": cstd, "ident": ident})
    res = _run(nc, maps)
    of1 = np.empty((T, GDN_VD), np.float32)
    ob1 = np.empty((T, GDN_VD), np.float32)
    for c in range(NCORES):
        gq, z = c // 2, c % 2
        if z == 0:
            of1[:, gq * 1024:(gq + 1) * 1024] = res[c]["o"]
        else:
            ob1[:, gq * 1024:(gq + 1) * 1024] = res[c]["o"][::-1]

    nc = _prog("E", lambda: build_phase_main(GDN_VD, GDN_DV, 0))
    shared = {"onw": rep128(np.tile(f(gdn_out_norm)[0], 512 // GDN_DV)), "w_out": _c(f(gdn_w_out)[0]),
              "ffn_norm": rep128(f(ffn_norm)[1]), "ffn_w_in": _c(f(ffn_w_in)[1]), "ffn_w_out": _c(f(ffn_w_out)[1]),
              "ple_norm": rep128(f(ple_norm)[1]), "ple_w_gate": _c(f(ple_w_gate)[1]),
              "ple_w_proj": _c(f(ple_w_proj)[1]), "next_norm": rep128(f(final_norm)), "ident": ident}
    maps = []
    for c in range(NCORES):
        m = dict(shared)
        m.update({"x": tok(h1, c), "of": tok(of1, c), "ob": tok(ob1, c), "og": tok(z1, c), "p": tok(p[1, 0], c)})
        maps.append(m)
    res = _run(nc, maps)
    out = np.concatenate([r["h_out"] for r in res], axis=0)
    return out[None].astype(np.float32)
```

```python
import contextlib
import numpy as np
import ml_dtypes
import concourse.bass as bass
import concourse.mybir as mybir
from concourse.bass_utils import run_bass_kernel_spmd

F32 = mybir.dt.float32
BF16 = mybir.dt.bfloat16
AF = mybir.ActivationFunctionType
ALU = mybir.AluOpType

NCORES = 8
T = 8192
D = 2048
TL = T // NCORES
EPS = 1e-6
D_FF = 5632
PLE = 256
GLA_H, GLA_DK, GLA_DV, GLA_R = 4, 256, 512, 16
GLA_KD, GLA_VD = 1024, 2048
GLA_PROJ = 2 * GLA_KD + 2 * GLA_VD + 2 * GLA_R
GDN_QKH, GDN_VH, GDN_DK, GDN_DV = 16, 32, 128, 128
GDN_KD, GDN_VD = 2048, 4096
GDN_CONV = 2 * GDN_KD + GDN_VD
GDN_PROJ = GDN_CONV + GDN_VD + 4 * GDN_VH
CONV_W = 5

SEM_EPOCH = 20000


class TT:
    __slots__ = ("name", "last_w", "readers", "sems", "cnts", "excl", "multi", "ws")

    def __init__(self, name):
        self.name = name
        self.excl = False
        self.multi = False
        self.ws = []
        self.last_w = None
        self.readers = []
        self.sems = {}
        self.cnts = {}


class Op:
    __slots__ = ("eng", "fn", "deps", "needs_inc", "sem", "val", "is_dma", "idx")

    def __init__(self, eng, fn, is_dma):
        self.eng = eng
        self.fn = fn
        self.deps = []
        self.needs_inc = False
        self.sem = None
        self.val = 0
        self.is_dma = is_dma
        self.idx = 0


COMPUTE = ("pe", "act", "dve", "pool")


class Sched:
    def __init__(self, nc, stack):
        self.nc = nc
        self.stack = stack
        self.ops = {"pe": [], "act": [], "dve": [], "pool": [], "sp": []}
        self.nops = 0
        self.dma_tiles = []
        self.fence_deps = []
        self.fence_pending = set()
        self.pid = {}
        self.free_slots = []
        self.all_slots = []
        self.nsem = 0

    def fence(self):
        deps = []
        for e in COMPUTE:
            for o in reversed(self.ops[e]):
                if not o.is_dma:
                    o.needs_inc = True
                    deps.append(o)
                    break
        for slot in self.all_slots:
            d = Op("sp", None, True)
            d.sem, d.val = slot[0], slot[1]
            deps.append(d)
        self.fence_deps = deps
        self.fence_pending = set(self.ops.keys())

    def _fence_apply(self, o):
        if o.eng in self.fence_pending:
            self.fence_pending.discard(o.eng)
            for d in self.fence_deps:
                if d.is_dma or d.eng != o.eng or o.is_dma:
                    o.deps.append(d)

    def tile(self, name):
        return TT(name)

    def _sem(self, name):
        return self.stack.enter_context(self.nc.semaphore(name))

    def _deps(self, op, reads, writes):
        ex = [r for r in reads if r.excl and r not in writes]
        if ex:
            reads = [r for r in reads if not r.excl]
            writes = list(writes) + ex
        cand = []
        for r in reads:
            if r.multi:
                cand.extend(r.ws)
            elif r.last_w is not None:
                cand.append(r.last_w)
        for w in writes:
            if w.last_w is not None and not w.multi:
                cand.append(w.last_w)
            cand.extend(w.readers)
        best = {}
        for d in cand:
            if d is op:
                continue
            if d.is_dma:
                key = ("dma", id(d.sem))
                if key not in best or best[key].val < d.val:
                    best[key] = d
            else:
                if d.eng == "pe" and op.eng == "pe" and not op.is_dma:
                    continue
                key = d.eng
                if key not in best or best[key].idx < d.idx:
                    best[key] = d
        for d in best.values():
            if not d.is_dma:
                d.needs_inc = True
            op.deps.append(d)
        for w in writes:
            if w.multi:
                w.ws.append(op)
            else:
                w.last_w = op
            w.readers = []
        for r in reads:
            if r.last_w is not op:
                r.readers.append(op)

    def op(self, eng, fn, reads=(), writes=()):
        o = Op(eng, fn, False)
        o.idx = self.nops
        self.nops += 1
        self._deps(o, reads, writes)
        self._fence_apply(o)
        self.ops[eng].append(o)
        return o

    def dma(self, eng, fn, tile, kind, reads=(), writes=(), n=1, inc=16):
        o = Op(eng, fn, True)
        o.idx = self.nops
        self.nops += 1
        if kind not in tile.sems:
            if self.free_slots and kind != "cc":
                slot = self.free_slots.pop()
            else:
                self.nsem += 1
                slot = [self._sem(f"d{self.nsem}"), 0]
                self.all_slots.append(slot)
            tile.sems[kind] = slot
            self.dma_tiles.append((tile, kind))
        slot = tile.sems[kind]
        self._deps(o, reads, writes)
        self._fence_apply(o)
        slot[1] += inc * n
        o.sem = slot[0]
        o.val = slot[1]
        self.ops[eng].append(o)
        return o

    def release_dma_sems(self):
        for (t, k) in self.dma_tiles:
            slot = t.sems.pop(k)
            if k != "cc":
                self.free_slots.append(slot)
        self.dma_tiles = []

    def emit(self):
        nc = self.nc
        for e in COMPUTE:
            cur, cnt, k = None, 0, 0
            for o in self.ops[e]:
                if o.is_dma or not o.needs_inc:
                    continue
                if cur is None or cnt >= SEM_EPOCH:
                    cur = self._sem(f"c_{e}_{k}")
                    k += 1
                    cnt = 0
                cnt += 1
                o.sem, o.val = cur, cnt
        finals = [(slot[0], slot[1]) for slot in self.all_slots]
        ops = self.ops

        def run(engname, eng, final=False):
            waited = {}
            if engname in ("sp", "pool"):
                p = eng.partition_id()
                self.pid[engname] = {"hq": eng.snap((p // 2) * 256), "p16": eng.snap(p * 16),
                                     "pTL": eng.snap(p * TL), "p256": eng.snap(p * 256), "p512": eng.snap(p * 512)}
            for oi, o in enumerate(ops[engname]):
                for d in o.deps:
                    key = id(d.sem)
                    if waited.get(key, 0) >= d.val:
                        continue
                    eng.wait_ge(d.sem, d.val)
                    waited[key] = d.val
                if o.is_dma:
                    o.fn(eng, o.sem)
                else:
                    ins = o.fn(eng)
                    if o.needs_inc:
                        ins.then_inc(o.sem, 1)
            if final:
                for sem, val in finals:
                    if waited.get(id(sem), 0) < val:
                        eng.wait_ge(sem, val)

        with nc.Block() as block:
            @block.sync
            def _(e):
                run("sp", e, final=True)

            @block.tensor
            def _(e):
                run("pe", e)

            @block.scalar
            def _(e):
                run("act", e)

            @block.vector
            def _(e):
                run("dve", e)

            @block.gpsimd
            def _(e):
                run("pool", e)


SB_BASE = 16512
SB_TOP = 229344


class Ctx:
    def __init__(self):
        self.nc = bass.Bass("TRN2", target_bir_lowering=False)
        self.stack = contextlib.ExitStack()
        self.s = Sched(self.nc, self.stack)
        self.n = 0
        self.off = SB_BASE
        self.phase_base = SB_BASE
        nc = self.nc
        self.banks = []
        for i in range(8):
            t = nc.alloc_psum_tensor(f"P_bank{i}", [128, 512], F32)
            tt = self.s.tile(f"bank{i}")
            tt.excl = True
            self.banks.append((t, tt))
        self.bank_i = 0

    def dram_in(self, name, shape, dt=F32):
        return self.nc.dram_tensor(name, list(shape), dt, kind="ExternalInput").ap()

    def dram_out(self, name, shape, dt=F32):
        return self.nc.dram_tensor(name, list(shape), dt, kind="ExternalOutput").ap()

    def dram(self, name, shape, dt=F32):
        t = self.nc.dram_tensor(name, list(shape), dt).ap()
        tt = self.s.tile(name)
        tt.multi = True
        return t, tt

    def sb(self, shape, dt, name=None):
        self.n += 1
        name = f"S{self.n}_" + (name or "t")
        esz = 2 if dt == BF16 else 4
        nbytes = int(np.prod(shape[1:])) * esz
        self.off = (self.off + 63) // 64 * 64
        assert self.off + nbytes <= SB_TOP, f"SBUF overflow allocating {name}: {self.off}+{nbytes}"
        t = self.nc.alloc_sbuf_tensor_at(name, list(shape), dt, offset=self.off)
        self.off += nbytes
        return t, self.s.tile(name)

    def ps(self, shape, dt, name=None):
        if dt == BF16:
            t, tt = self.banks[7]
            return t[:, :].bitcast(BF16), tt
        t, tt = self.banks[self.bank_i % 7]
        self.bank_i += 1
        return t, tt

    def phase_keep(self):
        self.phase_base = self.off

    def phase_end(self):
        self.s.fence()
        self.s.release_dma_sems()
        self.off = self.phase_base
        self.bank_i = 0

    def load(self, eng, dst_ap, src_ap, tile, reads=()):
        self.s.dma(eng, lambda e, sem: e.dma_start(out=dst_ap, in_=src_ap).then_inc(sem, 16),
                   tile, "ld", reads=list(reads), writes=[tile])

    def store(self, eng, dst_ap, src_ap, tile, writes=()):
        self.s.dma(eng, lambda e, sem: e.dma_start(out=dst_ap, in_=src_ap).then_inc(sem, 16),
                   tile, "st", reads=[tile], writes=list(writes))

    def allgather(self, src, src_t, dst, dst_t):
        def fn(e, sem):
            e.collective_compute("AllGather", ALU.bypass, replica_groups=[list(range(NCORES))],
                                 ins=[src], outs=[dst]).then_inc(sem)
        self.s.dma("pool", fn, dst_t, "cc", reads=[src_t], writes=[dst_t], n=1, inc=1)

    def finish(self):
        self.s.emit()
        self.stack.close()
        return self.nc


class Rot:
    def __init__(self, items):
        self.items = items
        self.i = 0

    def next(self):
        it = self.items[self.i % len(self.items)]
        self.i += 1
        return it


NT = TL // 128
KD = D // 128


class Dense:
    def __init__(self, cx, ident_d):
        self.cx = cx
        nc = cx.nc
        s = cx.s
        self.h, _ = cx.sb([128, NT, D], F32, "h")
        self.h_t = [s.tile(f"h{i}") for i in range(NT)]
        self.aT, _ = cx.sb([128, KD, TL], BF16, "aT")
        self.aT_t = [s.tile(f"aT{i}") for i in range(NT)]
        self.ident, self.ident_t = cx.sb([128, 128], BF16, "ident")
        cx.load("sp", self.ident[:], ident_d[:, :], self.ident_t)
        self.consts, self.consts_t = cx.sb([128, 4], F32, "consts")
        cx.s.op("dve", lambda e: e.memset(self.consts[:, 0:1], EPS), writes=[self.consts_t])
        self.xn = Rot([cx.sb([128, D], BF16, f"xn{i}") for i in range(1)])
        self.junk = Rot([cx.sb([128, D], BF16, f"junk{i}") for i in range(1)])
        self.gains = Rot([cx.sb([128, D], F32, f"gain{i}") for i in range(2)])
        self.stat = Rot([cx.sb([128, 8], F32, f"stat{i}") for i in range(4)])
        self.pst = Rot([cx.ps([128, 1024], BF16, f"pst{i}") for i in range(2)])
        self.psm = Rot([cx.ps([128, 512], F32, f"psm{i}") for i in range(5)])
        self.wp = Rot([cx.sb([128, 16 * 512], BF16, f"wp{i}") for i in range(3)])
        self.ev = Rot([cx.sb([128, 512], F32, f"ev{i}") for i in range(3)])

    def load_w(self, w_d, r0, kc, c0, ew):
        cx = self.cx
        wt, wtile = self.wp.next()
        dst = wt[:, 0:kc * ew].rearrange("p (k e) -> p k e", e=ew)
        nsplit = max(1, (kc + 7) // 8)
        per = (kc + nsplit - 1) // nsplit

        def fn(e, sem):
            for j in range(nsplit):
                k0, k1 = j * per, min(kc, (j + 1) * per)
                src = w_d[r0 + k0 * 128:r0 + k1 * 128, c0:c0 + ew].rearrange("(k p) e -> p k e", p=128)
                e.dma_start(out=dst[:, k0:k1, :], in_=src).then_inc(sem, 16)
        cx.s.dma("pool", fn, wtile, "ld", writes=[wtile], n=nsplit)
        return dst, wtile

    def rstd(self, src_ap, src_tiles, width, eps=EPS, mean=True):
        cx = self.cx
        junk, junk_t = self.junk.next()
        st, st_t = self.stat.next()
        eps_ap = self.consts[:, 0:1] if eps == EPS else float(eps)
        cx.s.op("act", lambda e: e.activation(out=junk[:, 0:width], in_=src_ap, func=AF.Square,
                                              accum_out=st[:, 0:1]),
                reads=list(src_tiles), writes=[junk_t, st_t])
        n = float(width) if mean else 1.0
        cx.s.op("act", lambda e: e.activation(out=st[:, 1:2], in_=st[:, 0:1], func=AF.Sqrt, bias=eps_ap,
                                              scale=1.0 / n), reads=[st_t, self.consts_t], writes=[st_t])
        cx.s.op("dve", lambda e: e.reciprocal(out=st[:, 2:3], in_=st[:, 1:2]), reads=[st_t], writes=[st_t])
        return st[:, 2:3], st_t

    def to_feat(self, xn, xn_t, i, ncols, k0=0):
        cx = self.cx
        nk = ncols // 128
        for g in range(0, nk, 8):
            ng = min(8, nk - g)
            pt, pt_t = self.pst.next()
            for j in range(ng):
                c = (g + j) * 128
                cx.s.op("pe", lambda e, j=j, c=c, pt=pt: e.transpose(out=pt[:, j * 128:(j + 1) * 128],
                                                                      in_=xn[:, c:c + 128],
                                                                      identity=self.ident[:]),
                        reads=[xn_t, self.ident_t], writes=[pt_t])
            dst = self.aT[:, k0 + g:k0 + g + ng, i * 128:(i + 1) * 128]
            src = pt[:, 0:ng * 128].rearrange("p (k t) -> p k t", t=128)
            eng = "act" if (g // 8) % 2 == 0 else "dve"
            if eng == "act":
                cx.s.op("act", lambda e, dst=dst, src=src: e.copy(out=dst, in_=src),
                        reads=[pt_t], writes=[self.aT_t[i]])
            else:
                cx.s.op("dve", lambda e, dst=dst, src=src: e.tensor_copy(out=dst, in_=src),
                        reads=[pt_t], writes=[self.aT_t[i]])

    def norm_to_feat(self, w_bc, w_bc_t):
        cx = self.cx
        for i in range(NT):
            r, r_t = self.rstd(self.h[:, i, :], [self.h_t[i]], D)
            xn, xn_t = self.xn.next()
            cx.s.op("dve", lambda e, i=i, r=r, xn=xn: e.scalar_tensor_tensor(
                out=xn[:], in0=self.h[:, i, :], scalar=r, in1=w_bc[:], op0=ALU.mult, op1=ALU.mult),
                reads=[self.h_t[i], r_t, w_bc_t], writes=[xn_t])
            self.to_feat(xn, xn_t, i, D)

    def mm_tok(self, w_d, kc, c0, c1, evac, r0=0, ew=512, k0=0):
        cx = self.cx
        for c in range(c0, c1, ew):
            w = min(ew, c1 - c)
            wt, wtile = self.load_w(w_d, r0, kc, c, w)
            for i in range(NT):
                ps, ps_t = self.psm.next()
                for k in range(kc):
                    cx.s.op("pe", lambda e, k=k, i=i, ps=ps, wt=wt, w=w: e.matmul(
                        ps[:, 0:w], lhsT=self.aT[:, k0 + k, i * 128:(i + 1) * 128], rhs=wt[:, k, 0:w],
                        start=(k == 0), stop=(k == kc - 1)),
                        reads=[self.aT_t[i], wtile], writes=[ps_t])
                evac(i, c, w, ps, ps_t)

    def load_bc(self, src_d, n, name):
        t, tt = self.cx.sb([128, n], F32, name)
        self.cx.load("sp", t[:], src_d[:, :], tt)
        return t, tt

    def load_h(self, x_d):
        for i in range(NT):
            self.cx.load("sp", self.h[:, i, :], x_d[i * 128:(i + 1) * 128, :], self.h_t[i])

    def proj_store(self, w_d, ncols, out_d):
        cx = self.cx

        def evac(i, c, w, ps, ps_t):
            ev, ev_t = self.ev.next()
            cx.s.op("act", lambda e: e.copy(out=ev[:, 0:w], in_=ps[:, 0:w]), reads=[ps_t], writes=[ev_t])
            cx.store("sp", out_d[i * 128:(i + 1) * 128, c:c + w], ev[:, 0:w], ev_t)
        self.mm_tok(w_d, KD, 0, ncols, evac)


def bf16_ident():
    return np.eye(128, dtype=np.float32).astype(ml_dtypes.bfloat16)


def rep128(v):
    v = np.asarray(v, np.float32).reshape(1, -1)
    return np.ascontiguousarray(np.broadcast_to(v, (128, v.shape[1])))


SC = 64
NSUB = 128 // SC
NLVL = 5
NEG = -30000.0
CB = 256
GB = 512
NCH = T // 128
G1R = 2 * GLA_KD + 2 * GLA_R


def bf16_ident():
    return np.eye(128, dtype=np.float32).astype(ml_dtypes.bfloat16)


def rep128(v):
    v = np.asarray(v, np.float32).reshape(1, -1)
    return np.ascontiguousarray(np.broadcast_to(v, (128, v.shape[1])))


def gla_consts(bwd):
    idx = np.arange(128)
    if bwd:
        tri = (idx[:, None] >= idx[None, :]).astype(np.float32)
    else:
        tri = (idx[:, None] <= idx[None, :]).astype(np.float32)
    triS = tri * (-1.0 / 16.0)
    triUS = (1.0 - tri) * (-1.0 / 16.0)
    return np.concatenate([triS, triUS, tri], axis=1).astype(np.float32)


def gdn_consts(bwd):
    idx = np.arange(128)
    same = (idx[:, None] // SC) == (idx[None, :] // SC)
    ident = np.eye(128, dtype=np.float32)
    ones = np.ones((128, 128), np.float32)
    if bwd:
        le = idx[:, None] >= idx[None, :]
        lt = idx[:, None] > idx[None, :]
    else:
        le = idx[:, None] <= idx[None, :]
        lt = idx[:, None] < idx[None, :]
    tri = (le & same).astype(np.float32)
    nms = np.where(lt & same, 0.0, NEG).astype(np.float32)
    nmi = np.where(le & same, 0.0, NEG).astype(np.float32)
    blk = same.astype(np.float32)
    brow = [np.broadcast_to(((idx // SC) == b)[:, None], (128, 128)).astype(np.float32) for b in range(NSUB)]
    return np.concatenate([ident, ones, tri, nms, nmi, blk] + brow, axis=1).astype(np.float32)


def dense_feat_store(dn, w_d, c0, c1, out_d, out_t, orow0):
    cx, s = dn.cx, dn.cx.s
    for c in range(c0, c1, 512):
        w = min(512, c1 - c)
        wt, wtile = dn.load_w(w_d, 0, KD, c, w)
        for j in range(0, w, 128):
            m = min(128, w - j)
            for th in range(TL // 512):
                ts_ = slice(th * 512, (th + 1) * 512)
                rt = [dn.aT_t[q] for q in range(th * 4, th * 4 + 4)]
                ps, ps_t = dn.psm.next()
                for k in range(KD):
                    s.op("pe", lambda e, ps=ps, wt=wt, k=k, j=j, m=m, ts_=ts_: e.matmul(
                        ps[0:m, :], lhsT=wt[:, k, j:j + m], rhs=dn.aT[:, k, ts_], start=(k == 0), stop=(k == KD - 1)),
                        reads=rt + [wtile], writes=[ps_t])
                ev, ev_t = dn.ev.next()
                s.op("act", lambda e, ev=ev, ps=ps, m=m: e.copy(out=ev[0:m, :], in_=ps[0:m, :]), reads=[ps_t],
                     writes=[ev_t])
                r0 = orow0 + (c - c0) + j
                cx.store("sp", out_d[r0:r0 + m, ts_], ev[0:m, :], ev_t, writes=[out_t])


def dense_tok_store(dn, w_d, c0, c1, out_d, out_t, oc0):
    cx, s = dn.cx, dn.cx.s

    def evac(i, c, w, ps, ps_t):
        ev, ev_t = dn.ev.next()
        s.op("act", lambda e: e.copy(out=ev[:, 0:w], in_=ps[:, 0:w]), reads=[ps_t], writes=[ev_t])
        cx.store("sp", out_d[i * 128:(i + 1) * 128, oc0 + c - c0:oc0 + c - c0 + w], ev[:, 0:w], ev_t, writes=[out_t])
    dn.mm_tok(w_d, KD, c0, c1, evac)


def gla_phase(cx, G1, G1_t, G2, G2_t, wgu_d, bb_d, gcst_d, src3, src3_t):
    s = cx.s
    DVH = GLA_DV // 2
    scale = float(GLA_DK) ** -0.5
    cst, cst_t = cx.sb([128, 2 * 384], F32, "gcst")
    cx.load("sp", cst[:], gcst_d[:, :], cst_t)
    wg, wg_t = cx.sb([GLA_R, 2 * GLA_DK], F32, "wg")
    for z in range(2):
        cx.load("sp", wg[:, z * GLA_DK:(z + 1) * GLA_DK], wgu_d[z, :, :], wg_t)
    bb, bb_t = cx.sb([128, 2 * GLA_DK], F32, "bb")
    cx.load("sp", bb[:], bb_d[:, :], bb_t)
    oacc, _ = cx.sb([128, NCH, DVH], F32, "oacc")
    oacc_t = [s.tile(f"oacc{n}") for n in range(NCH)]
    ps = Rot([cx.ps([128, 512], F32) for i in range(7)])
    pid = s.pid

    D_ = {}
    for z in range(2):
        d = {}
        d["qTb"] = Rot([cx.sb([128, 2, GB], F32, f"qTb{z}{i}") for i in range(2)])
        d["kTb"] = Rot([cx.sb([128, 2, GB], F32, f"kTb{z}{i}") for i in range(2)])
        d["kb"] = Rot([cx.sb([128, GB // 128, GLA_DK], BF16, f"kb{z}{i}") for i in range(2)])
        d["vb"] = Rot([cx.sb([128, GB // 128, DVH], BF16, f"vb{z}{i}") for i in range(2)])
        d["lrb"] = Rot([cx.sb([GLA_R, GB], F32, f"lrb{z}{i}") for i in range(2)])
        d["Sf"] = cx.sb([128, 2, DVH], F32, f"Sf{z}")
        d["Sb"] = cx.sb([128, 2, DVH], BF16, f"Sb{z}")
        s.op("dve", lambda e, d=d: e.memset(d["Sf"][0][:], 0.0), writes=[d["Sf"][1]])
        s.op("pool", lambda e, d=d: e.memset(d["Sb"][0][:], 0.0), writes=[d["Sb"][1]])
        for nm, shp, dt_ in (("gk", [128, GLA_DK], F32), ("sp", [128, GLA_DK], F32), ("kds", [128, GLA_DK], F32),
                             ("kdec", [128, GLA_DK], BF16), ("eq", [128, 2, 128], F32), ("en", [128, 2, 128], F32),
                             ("qdT", [128, 2, 128], BF16), ("kiT", [128, 2, 128], BF16), ("AT", [128, 128], BF16)):
            d[nm] = Rot([cx.sb(shp, dt_, f"{nm}{z}{i}") for i in range(2)])
        d["blk"] = None
        d["cur"] = None
        D_[z] = d

    Lq, Lq_t = cx.dram("Lq", [GLA_DK, T])
    Lkf, Lkf_t = cx.dram("Lkf", [GLA_DK, T])
    Lk, Lk_t = cx.dram("Lk", [T, GLA_DK])
    Lv, Lv_t = cx.dram("Lv", [T, DVH])
    G1v = G1.rearrange("(r f) t -> r f t", f=G1R)

    def loc(eng, dst_ap, src_fn, dst_t, src_t):
        s.dma(eng, lambda e, sem: e.dma_start(out=dst_ap, in_=src_fn(pid[eng])).then_inc(sem, 16), dst_t, "loc",
              reads=[src_t], writes=[dst_t])
    loc("sp", Lq.rearrange("f (r t) -> r f t", t=TL), lambda p: G1v[:, bass.ds(p["hq"], GLA_DK), :], Lq_t, G1_t)
    loc("sp", Lkf.rearrange("f (r t) -> r f t", t=TL), lambda p: G1v[:, GLA_KD:2 * GLA_KD, :][:, bass.ds(p["hq"], GLA_DK), :],
        Lkf_t, G1_t)
    loc("pool", Lk[:, :], lambda p: G2[:, bass.ds(p["hq"], GLA_DK)], Lk_t, G2_t)
    loc("pool", Lv[:, :], lambda p: G2[:, GLA_KD:GLA_KD + GLA_VD][:, bass.ds(p["p256"], DVH)], Lv_t, G2_t)

    def load_block(z, b):
        d = D_[z]
        t0 = b * GB
        r, tl0 = t0 // TL, t0 % TL
        qt, qt_t = d["qTb"].next()
        kt, kt_t = d["kTb"].next()
        kk, kk_t = d["kb"].next()
        vv, vv_t = d["vb"].next()
        lr, lr_t = d["lrb"].next()
        cx.load("sp", qt[:], Lq[:, t0:t0 + GB].rearrange("(c p) t -> p c t", p=128), qt_t, reads=[Lq_t])
        cx.load("sp", kt[:], Lkf[:, t0:t0 + GB].rearrange("(c p) t -> p c t", p=128), kt_t, reads=[Lkf_t])
        cx.load("pool", kk[:], Lk[t0:t0 + GB, :].rearrange("(c p) d -> p c d", p=128), kk_t, reads=[Lk_t])
        cx.load("pool", vv[:], Lv[t0:t0 + GB, :].rearrange("(c p) d -> p c d", p=128), vv_t, reads=[Lv_t])
        lr0 = r * G1R + 2 * GLA_KD + z * GLA_R
        cx.load("sp", lr[:], G1[lr0:lr0 + GLA_R, tl0:tl0 + GB], lr_t, reads=[G1_t])
        d["blk"] = (qt, qt_t, kt, kt_t, kk, kk_t, vv, vv_t, lr, lr_t)
        d["cur"] = b

    def chunk(z, n, first):
        d = D_[z]
        b, c = n // (GB // 128), n % (GB // 128)
        if d["cur"] != b:
            load_block(z, b)
        qt, qt_t, kt, kt_t, kk, kk_t, vv, vv_t, lr, lr_t = d["blk"]
        cs = slice(c * 128, (c + 1) * 128)
        triS, triUS, maskU = (cst[:, z * 384 + i * 128:z * 384 + (i + 1) * 128] for i in range(3))
        wgz = wg[:, z * GLA_DK:(z + 1) * GLA_DK]
        bbz = bb[:, z * GLA_DK:(z + 1) * GLA_DK]
        lastc = 0 if z else 127
        Sf, Sf_t = d["Sf"]
        Sb, Sb_t = d["Sb"]
        p1, p1_t = ps.next()
        s.op("pe", lambda e: e.matmul(p1[:, 0:GLA_DK], lhsT=lr[:, cs], rhs=wgz, start=True, stop=True),
             reads=[lr_t, wg_t], writes=[p1_t])
        g1, g1_t = d["gk"].next()
        s.op("dve", lambda e: e.tensor_tensor(out=g1[:], in0=p1[:, 0:GLA_DK], in1=bbz, op=ALU.add),
             reads=[p1_t, bb_t], writes=[g1_t])
        s1, s1_t = d["sp"].next()
        s.op("act", lambda e: e.activation(out=g1[:], in_=g1[:], func=AF.Exp, scale=-1.0), reads=[g1_t], writes=[g1_t])
        s.op("act", lambda e: e.activation(out=s1[:], in_=g1[:], func=AF.Ln, bias=1.0), reads=[g1_t], writes=[s1_t])
        p2, p2_t = ps.next()
        s.op("pe", lambda e: e.matmul(p2[:, 0:GLA_DK], lhsT=triUS, rhs=s1[:], start=True, stop=True),
             reads=[cst_t, s1_t], writes=[p2_t])
        p3, p3_t = ps.next()
        for dc in range(2):
            s.op("pe", lambda e, dc=dc: e.matmul(p3[:, dc * 128:(dc + 1) * 128], lhsT=s1[:, dc * 128:(dc + 1) * 128],
                                                 rhs=triS, start=True, stop=True),
                 reads=[cst_t, s1_t], writes=[p3_t])
        kd, kd_t = d["kds"].next()
        s.op("act", lambda e: e.activation(out=kd[:], in_=p2[:, 0:GLA_DK], func=AF.Exp), reads=[p2_t], writes=[kd_t])
        kdc, kdc_t = d["kdec"].next()
        s.op("pool", lambda e: e.tensor_tensor(out=kdc[:], in0=kk[:, c, :], in1=kd[:], op=ALU.mult),
             reads=[kk_t, kd_t], writes=[kdc_t])
        e1, e1_t = d["eq"].next()
        e2, e2_t = d["en"].next()
        p3v = p3[:, 0:256].rearrange("p (c t) -> p c t", t=128)
        s.op("act", lambda e: e.activation(out=e1[:], in_=p3v, func=AF.Exp), reads=[p3_t], writes=[e1_t])
        s.op("act", lambda e: e.activation(out=e2[:], in_=p3v, func=AF.Exp, scale=-1.0), reads=[p3_t], writes=[e2_t])
        qd, qd_t = d["qdT"].next()
        ki, ki_t = d["kiT"].next()
        s.op("dve", lambda e: e.scalar_tensor_tensor(out=qd[:], in0=qt[:, :, cs], scalar=scale, in1=e1[:],
                                                     op0=ALU.mult, op1=ALU.mult), reads=[qt_t, e1_t], writes=[qd_t])
        s.op("dve", lambda e: e.tensor_tensor(out=ki[:], in0=kt[:, :, cs], in1=e2[:], op=ALU.mult),
             reads=[kt_t, e2_t], writes=[ki_t])
        p4, p4_t = ps.next()
        for dc in range(2):
            s.op("pe", lambda e, dc=dc: e.matmul(p4[:, 0:128], lhsT=ki[:, dc, :], rhs=qd[:, dc, :], start=(dc == 0),
                                                 stop=(dc == 1)), reads=[ki_t, qd_t], writes=[p4_t])
        at, at_t = d["AT"].next()
        s.op("dve", lambda e: e.tensor_tensor(out=at[:], in0=p4[:, 0:128], in1=maskU, op=ALU.mult),
             reads=[p4_t, cst_t], writes=[at_t])
        p5, p5_t = ps.next()
        s.op("pe", lambda e: e.matmul(p5[:, 0:DVH], lhsT=at[:], rhs=vv[:, c, :], start=True, stop=False),
             reads=[at_t, vv_t], writes=[p5_t])
        for dc in range(2):
            s.op("pe", lambda e, dc=dc: e.matmul(p5[:, 0:DVH], lhsT=qd[:, dc, :], rhs=Sb[:, dc, :], start=False,
                                                 stop=(dc == 1)), reads=[qd_t, Sb_t], writes=[p5_t])
        if first:
            s.op("act", lambda e: e.copy(out=oacc[:, n, :], in_=p5[:, 0:DVH]), reads=[p5_t], writes=[oacc_t[n]])
        else:
            s.op("dve", lambda e: e.tensor_tensor(out=oacc[:, n, :], in0=p5[:, 0:DVH], in1=oacc[:, n, :], op=ALU.add),
                 reads=[p5_t, oacc_t[n]], writes=[oacc_t[n]])
        p6, p6_t = ps.next()
        for dc in range(2):
            s.op("pe", lambda e, dc=dc: e.matmul(p6[:, dc * DVH:(dc + 1) * DVH], lhsT=kdc[:, dc * 128:(dc + 1) * 128],
                                                 rhs=vv[:, c, :], start=True, stop=True),
                 reads=[kdc_t, vv_t], writes=[p6_t])
        for dc in range(2):
            s.op("dve", lambda e, dc=dc: e.scalar_tensor_tensor(
                out=Sf[:, dc, :], in0=Sf[:, dc, :], scalar=e1[:, dc, lastc:lastc + 1],
                in1=p6[:, dc * DVH:(dc + 1) * DVH], op0=ALU.mult, op1=ALU.add),
                reads=[Sf_t, e1_t, p6_t], writes=[Sf_t])
        s.op("act", lambda e: e.copy(out=Sb[:], in_=Sf[:]), reads=[Sf_t], writes=[Sb_t])

    for k in range(NCH):
        first = k < NCH // 2
        chunk(0, k, first)
        chunk(1, NCH - 1 - k, first)
    st_t = s.tile("oacc_st")
    src3v = src3.rearrange("(n p) d -> p n d", p=128)
    for n0 in range(0, NCH, 16):
        s.dma("sp", lambda e, sem, n0=n0: e.dma_start(out=src3v[:, n0:n0 + 16, :], in_=oacc[:, n0:n0 + 16, :])
              .then_inc(sem, 16), st_t, "st", reads=oacc_t[n0:n0 + 16], writes=[src3_t])


def gdn_phase(cx, G4, G4_t, G5, G5_t, cw_d, hp_d, dcst_d, idb_d, qkn, qkn_t, src6, src6_t):
    s = cx.s
    pid = s.pid
    W4 = NCH * 4
    cstA, cstA_t = cx.sb([128, 2 * 8 * 128], F32, "dcst")
    cx.load("sp", cstA[:], dcst_d[:, :], cstA_t)
    idb, idb_t = cx.sb([128, 128], BF16, "identb")
    cx.load("sp", idb[:], idb_d[:, :], idb_t)
    cw, cw_t = cx.sb([128, 8 * CONV_W], F32, "cw")
    cx.load("sp", cw[:], cw_d[:, :], cw_t)
    hp, hp_t = cx.sb([128, 4 * W4], F32, "hp")
    cx.load("sp", hp[:], hp_d[:, :], hp_t)
    ba, ba_t = cx.sb([128, NCH, 16], F32, "ba")
    Lx, Lx_t = cx.dram("Lx", [8 * 128, T])
    Lba, Lba_t = cx.dram("Lba", [T, 16])
    G4v = G4.rearrange("(r f) t -> r f t", f=GDN_CONV)
    Lxv = Lx.rearrange("f (r t) -> r f t", t=TL)

    def loc(eng, dst_ap, src_fn, dst_t, src_t):
        s.dma(eng, lambda e, sem: e.dma_start(out=dst_ap, in_=src_fn(pid[eng])).then_inc(sem, 16), dst_t, "loc",
              reads=[src_t], writes=[dst_t])
    loc("pool", Lxv[:, 0:256, :], lambda p: G4v[:, bass.ds(p["p256"], 256), :], Lx_t, G4_t)
    loc("pool", Lxv[:, 256:512, :], lambda p: G4v[:, GDN_KD:2 * GDN_KD, :][:, bass.ds(p["p256"], 256), :], Lx_t, G4_t)
    loc("pool", Lxv[:, 512:1024, :], lambda p: G4v[:, 2 * GDN_KD:GDN_CONV, :][:, bass.ds(p["p512"], 512), :], Lx_t, G4_t)
    loc("sp", Lba[:, :], lambda p: G5[:, bass.ds(p["p16"], 16)], Lba_t, G5_t)
    cx.load("sp", ba[:], Lba[:, :].rearrange("(n p) c -> p n c", p=128), ba_t, reads=[Lba_t])
    epsc, epsc_t = cx.sb([128, 1], F32, "epsc")
    s.op("dve", lambda e: e.memset(epsc[:], EPS), writes=[epsc_t])
    psw = Rot([cx.ps([128, 512], F32) for i in range(2)])
    pss = Rot([cx.ps([128, 512], F32) for i in range(2)])
    pscan = Rot([cx.ps([128, 512], F32) for i in range(3)])
    pstb = Rot([cx.ps([128, 1024], BF16)])

    def consts(z):
        c_ = [cstA[:, z * 1024 + i * 128:z * 1024 + (i + 1) * 128] for i in range(8)]
        return c_[0], c_[1], c_[2], c_[3], c_[4], c_[5], c_[6:8]

    GT = {}
    for z in range(2):
        ident, ones, tri, nms, nmi, blk, brow = consts(z)
        gt = lambda name: cx.sb([128, W4], F32, f"{name}{z}")
        g, g_t = gt("g")
        l2, l2_t = gt("l2")
        beta, beta_t = gt("beta")
        tmpw, tmpw_t = gt("tmpw")
        c_sb, c_t = gt("c_sb")
        negc, negc_t = gt("negc")
        expc, expc_t = gt("expc")
        clb, clb_t = gt("clb")
        bexpc, bexpc_t = gt("bexpc")
        kdsc, kdsc_t = gt("kdsc")
        cdb = [gt(f"cdb{b}") for b in range(NSUB)]
        bv3 = ba[:, :, z * 4:z * 4 + 4]
        av3 = ba[:, :, 8 + z * 4:8 + z * 4 + 4]
        as3 = lambda ap: ap.rearrange("p (n h) -> p n h", h=4)
        alog = hp[:, (2 * z) * W4:(2 * z + 1) * W4]
        dtb = hp[:, (2 * z + 1) * W4:(2 * z + 2) * W4]
        s.op("dve", lambda e, tmpw=tmpw, av3=av3, dtb=dtb, as3=as3: e.tensor_tensor(
            out=as3(tmpw[:]), in0=av3, in1=as3(dtb), op=ALU.add), reads=[ba_t, hp_t], writes=[tmpw_t])
        s.op("act", lambda e, tmpw=tmpw: e.activation(out=tmpw[:], in_=tmpw[:], func=AF.Exp), reads=[tmpw_t],
             writes=[tmpw_t])
        s.op("act", lambda e, tmpw=tmpw: e.activation(out=tmpw[:], in_=tmpw[:], func=AF.Ln, bias=1.0), reads=[tmpw_t],
             writes=[tmpw_t])
        s.op("act", lambda e, g=g, alog=alog: e.activation(out=g[:], in_=alog, func=AF.Exp), reads=[hp_t], writes=[g_t])
        s.op("dve", lambda e, g=g, tmpw=tmpw: e.scalar_tensor_tensor(out=g[:], in0=tmpw[:], scalar=-1.0, in1=g[:],
                                                                     op0=ALU.mult, op1=ALU.mult),
             reads=[tmpw_t, g_t], writes=[g_t])
        s.op("act", lambda e, l2=l2, bv3=bv3, as3=as3: e.activation(out=as3(l2[:]), in_=bv3, func=AF.Exp, scale=-1.0),
             reads=[ba_t], writes=[l2_t])
        s.op("act", lambda e, l2=l2: e.activation(out=l2[:], in_=l2[:], func=AF.Ln, bias=1.0), reads=[l2_t],
             writes=[l2_t])
        s.op("act", lambda e, l2=l2, beta=beta: e.activation(out=beta[:], in_=l2[:], func=AF.Exp, scale=-1.0),
             reads=[l2_t], writes=[beta_t])
        pc, pc_t = psw.next()
        s.op("pe", lambda e, pc=pc, tri=tri, g=g: e.matmul(pc[:, 0:W4], lhsT=tri, rhs=g[:], start=True, stop=True),
             reads=[cstA_t, g_t], writes=[pc_t])
        s.op("act", lambda e, pc=pc, c_sb=c_sb: e.copy(out=c_sb[:], in_=pc[:, 0:W4]), reads=[pc_t], writes=[c_t])
        s.op("act", lambda e, pc=pc, expc=expc: e.activation(out=expc[:], in_=pc[:, 0:W4], func=AF.Exp), reads=[pc_t],
             writes=[expc_t])
        s.op("dve", lambda e, pc=pc, negc=negc: e.tensor_scalar(out=negc[:], in0=pc[:, 0:W4], scalar1=-1.0,
                                                                scalar2=None, op0=ALU.mult),
             reads=[pc_t], writes=[negc_t])
        s.op("dve", lambda e, pc=pc, clb=clb, l2=l2: e.tensor_tensor(out=clb[:], in0=pc[:, 0:W4], in1=l2[:],
                                                                     op=ALU.subtract),
             reads=[pc_t, l2_t], writes=[clb_t])
        s.op("act", lambda e, bexpc=bexpc, clb=clb: e.activation(out=bexpc[:], in_=clb[:], func=AF.Exp), reads=[clb_t],
             writes=[bexpc_t])
        pl, pl_t = psw.next()
        s.op("pe", lambda e, pl=pl, blk=blk, g=g: e.matmul(pl[:, 0:W4], lhsT=blk, rhs=g[:], start=True, stop=True),
             reads=[cstA_t, g_t], writes=[pl_t])
        s.op("dve", lambda e, pl=pl, kdsc=kdsc, c_sb=c_sb: e.tensor_tensor(out=kdsc[:], in0=pl[:, 0:W4], in1=c_sb[:],
                                                                           op=ALU.subtract),
             reads=[pl_t, c_t], writes=[kdsc_t])
        s.op("act", lambda e, kdsc=kdsc: e.activation(out=kdsc[:], in_=kdsc[:], func=AF.Exp), reads=[kdsc_t],
             writes=[kdsc_t])
        for b in range(NSUB):
            pb, pb_t = psw.next()
            s.op("pe", lambda e, pb=pb, b=b, brow=brow, g=g: e.matmul(pb[:, 0:W4], lhsT=brow[b], rhs=g[:], start=True,
                                                                      stop=True),
                 reads=[cstA_t, g_t], writes=[pb_t])
            s.op("act", lambda e, pb=pb, b=b, cdb=cdb: e.activation(out=cdb[b][0][:], in_=pb[:, 0:W4], func=AF.Exp),
                 reads=[pb_t], writes=[cdb[b][1]])
        GT[z] = dict(beta=(beta, beta_t), c_sb=(c_sb, c_t), negc=(negc, negc_t), expc=(expc, expc_t),
                     clb=(clb, clb_t), bexpc=(bexpc, bexpc_t), kdsc=(kdsc, kdsc_t), cdb=cdb)
    cx.phase_keep_local = cx.off

    base_off = cx.off
    xin = Rot([cx.sb([128, 8, CB + 4], F32, f"xin{i}") for i in range(2)])
    acc = Rot([cx.sb([128, CB], F32, f"acc{i}") for i in range(4)])
    acc2 = Rot([cx.sb([128, CB], F32, f"acc2{i}") for i in range(2)])
    sl = Rot([cx.sb([128, CB], F32, f"sl{i}") for i in range(3)])
    sq = Rot([cx.sb([128, CB], F32, f"sq{i}") for i in range(2)])
    rinv = Rot([cx.sb([128, CB], F32, f"rinv{i}") for i in range(2)])
    qkv = Rot([cx.sb([128, 8, CB], BF16, f"qkv{i}") for i in range(2)])
    ones0 = consts(0)[1]
    nblk = T // CB
    qknv = qkn.rearrange("(j p) t -> p j t", p=128)
    for bi in range(nblk):
        t0 = bi * CB
        xi, xi_t = xin.next()
        if bi == 0:
            s.op("pool", lambda e, xi=xi: e.memset(xi[:, :, 0:2], 0.0), writes=[xi_t])
        if bi == nblk - 1:
            s.op("pool", lambda e, xi=xi: e.memset(xi[:, :, CB + 2:CB + 4], 0.0), writes=[xi_t])
        lo = 2 if bi == 0 else 0
        hi = CB + 2 if bi == nblk - 1 else CB + 4
        cx.load("sp", xi[:, :, lo:hi], Lx[:, t0 - 2 + lo:t0 - 2 + hi].rearrange("(j p) t -> p j t", p=128), xi_t,
                reads=[Lx_t])
        qk, qk_t = qkv.next()
        for j in range(8):
            eng = "dve" if j % 2 == 0 else "pool"
            a_, a_t = acc.next()
            s.op(eng, lambda e, a_=a_, xi=xi, j=j: e.tensor_scalar(
                out=a_[:], in0=xi[:, j, 0:CB], scalar1=cw[:, j * CONV_W:j * CONV_W + 1], scalar2=None, op0=ALU.mult),
                reads=[xi_t, cw_t], writes=[a_t])
            for k in range(1, CONV_W):
                if eng == "dve":
                    s.op(eng, lambda e, a_=a_, xi=xi, j=j, k=k: e.scalar_tensor_tensor(
                        out=a_[:], in0=xi[:, j, k:k + CB], scalar=cw[:, j * CONV_W + k:j * CONV_W + k + 1], in1=a_[:],
                        op0=ALU.mult, op1=ALU.add), reads=[xi_t, cw_t, a_t], writes=[a_t])
                else:
                    a2, a2_t = acc2.next()
                    s.op(eng, lambda e, a2=a2, xi=xi, j=j, k=k: e.tensor_scalar(
                        out=a2[:], in0=xi[:, j, k:k + CB], scalar1=cw[:, j * CONV_W + k:j * CONV_W + k + 1],
                        scalar2=None, op0=ALU.mult), reads=[xi_t, cw_t], writes=[a2_t])
                    s.op(eng, lambda e, a_=a_, a2=a2: e.tensor_tensor(out=a_[:], in0=a_[:], in1=a2[:], op=ALU.add),
                         reads=[a_t, a2_t], writes=[a_t])
            if j >= 4:
                s.op("act", lambda e, a_=a_, qk=qk, j=j: e.activation(out=qk[:, j, :], in_=a_[:], func=AF.Silu),
                     reads=[a_t], writes=[qk_t])
                continue
            sl1, sl1_t = sl.next()
            s.op("act", lambda e, a_=a_, sl1=sl1: e.activation(out=sl1[:], in_=a_[:], func=AF.Silu), reads=[a_t],
                 writes=[sl1_t])
            sq1, sq1_t = sq.next()
            s.op("pool", lambda e, sq1=sq1, sl1=sl1: e.tensor_tensor(out=sq1[:], in0=sl1[:], in1=sl1[:], op=ALU.mult),
                 reads=[sl1_t], writes=[sq1_t])
            pw, pw_t = psw.next()
            s.op("pe", lambda e, pw=pw, sq1=sq1: e.matmul(pw[:, 0:CB], lhsT=ones0, rhs=sq1[:], start=True, stop=True),
                 reads=[cstA_t, sq1_t], writes=[pw_t])
            ri, ri_t = rinv.next()
            s.op("act", lambda e, ri=ri, pw=pw: e.activation(out=ri[:], in_=pw[:, 0:CB], func=AF.Ln, bias=epsc[:, 0:1]),
                 reads=[pw_t, epsc_t], writes=[ri_t])
            s.op("act", lambda e, ri=ri: e.activation(out=ri[:], in_=ri[:], func=AF.Exp, scale=-0.5), reads=[ri_t],
                 writes=[ri_t])
            sc_ = float(GDN_DK) ** -0.5 if j < 2 else 1.0
            s.op("dve", lambda e, qk=qk, j=j, sl1=sl1, ri=ri, sc_=sc_: e.scalar_tensor_tensor(
                out=qk[:, j, :], in0=sl1[:], scalar=sc_, in1=ri[:], op0=ALU.mult, op1=ALU.mult),
                reads=[sl1_t, ri_t], writes=[qk_t])
        cx.store("sp", qknv[:, :, t0:t0 + CB], qk[:], qk_t, writes=[qkn_t])
    s.fence()
    cx.off = base_off

    G = Rot([cx.sb([128, 256], F32, f"G{i}") for i in range(8)])

    def gb(nm, dt):
        return [cx.sb([128, 4, 128], dt, f"{nm}{g_}") for g_ in range(2)]

    def gb2(nm, dt):
        return [[cx.sb([128, 4, 128], dt, f"{nm}{g_}{q_}") for q_ in range(2)] for g_ in range(2)]
    Sf, tmpo = gb("Sf", F32), gb("tmpo", F32)
    Sb, TTb, bvb, kbd, vnew = (gb(n_, BF16) for n_ in ("Sb", "TT", "bv", "kbd", "vn"))
    ub2 = gb2("u", F32)
    kdc2, attn2, wTb2 = (gb2(n_, BF16) for n_ in ("kdc", "attn", "wT"))
    for g_ in range(2):
        s.op("dve", lambda e, g_=g_: e.memset(Sf[g_][0][:], 0.0), writes=[Sf[g_][1]])
        s.op("pool", lambda e, g_=g_: e.memset(Sb[g_][0][:], 0.0), writes=[Sb[g_][1]])
        s.op("pool", lambda e, g_=g_: e.memset(vnew[g_][0][:], 0.0), writes=[vnew[g_][1]])
    f4 = Rot([cx.sb([128, 4, 128], F32, f"f4_{i}") for i in range(12)])
    E1 = Rot([cx.sb([128, 4, 128], F32, f"E1_{i}") for i in range(2)])
    E2 = Rot([cx.sb([128, 4, 128], F32, f"E2_{i}") for i in range(2)])
    dg = Rot([cx.sb([128, 4, 128], F32, f"dg{i}") for i in range(4)])
    osb = Rot([cx.sb([128, 512], F32, f"osb{i}") for i in range(4)])
    oold = Rot([cx.sb([128, 512], F32, f"oold{i}") for i in range(2)])
    qkc = [Rot([cx.sb([128, 8, 128], BF16, f"qkc{g_}{i}") for i in range(3)]) for g_ in range(2)]
    o_tiles = [s.tile(f"o6_{n}") for n in range(NCH)]
    v4 = lambda ap: ap.rearrange("p (h t) -> p h t", t=128)

    def prep_steps(gi, n, par, qk, qk_t, Gs):
        hs = range(4)
        ub, kdc, attn, wTb = ([None, None] for _ in range(4))
        ub[gi], kdc[gi], attn[gi], wTb[gi] = ub2[gi][par], kdc2[gi][par], attn2[gi][par], wTb2[gi][par]
        ident, ones, tri, nms, nmi, blk, brow = consts(gi)
        gtz = GT[gi]
        wcol = lambda key, h: gtz[key][0][:, n * 4 + h:n * 4 + h + 1]
        tl = lambda key: gtz[key][1]
        pt, pt_t = pstb.next()
        for h in hs:
            s.op("pe", lambda e, h=h: e.transpose(out=pt[:, h * 128:(h + 1) * 128], in_=qk[:, 4 + h, :],
                                                  identity=idb[:]), reads=[qk_t, idb_t], writes=[pt_t])
        for q in range(2):
            s.op("pe", lambda e, q=q: e.transpose(out=pt[:, (4 + q) * 128:(5 + q) * 128], in_=qk[:, 2 + q, :],
                                                  identity=idb[:]), reads=[qk_t, idb_t], writes=[pt_t])
        for h in hs:
            kk = pt[:, (4 + h // 2) * 128:(5 + h // 2) * 128]
            s.op("dve", lambda e, h=h: e.tensor_scalar(out=bvb[gi][0][:, h, :], in0=pt[:, h * 128:(h + 1) * 128],
                                                       scalar1=wcol("beta", h), scalar2=None, op0=ALU.mult),
                 reads=[pt_t, tl("beta")], writes=[bvb[gi][1]])
            s.op("dve", lambda e, h=h, kk=kk: e.tensor_scalar(out=kbd[gi][0][:, h, :], in0=kk,
                                                              scalar1=wcol("bexpc", h), scalar2=None, op0=ALU.mult),
                 reads=[pt_t, tl("bexpc")], writes=[kbd[gi][1]])
            s.op("dve", lambda e, h=h, kk=kk: e.tensor_scalar(out=kdc[gi][0][:, h, :], in0=kk,
                                                              scalar1=wcol("kdsc", h), scalar2=None, op0=ALU.mult),
                 reads=[pt_t, tl("kdsc")], writes=[kdc[gi][1]])
        yield
        Es = []
        for (key, nm, Epool) in (("clb", nms, E1), ("c_sb", nmi, E2)):
            d1, d1_t = dg.next()
            for h in hs:
                s.op("pool", lambda e, h=h, d1=d1, key=key: e.tensor_scalar(
                    out=d1[:, h, :], in0=ident, scalar1=wcol(key, h), scalar2=None, op0=ALU.mult),
                    reads=[cstA_t, tl(key)], writes=[d1_t])
            pe_, pe_t = pss.next()
            for h in hs:
                s.op("pe", lambda e, h=h, pe_=pe_, d1=d1: e.matmul(pe_[:, h * 128:(h + 1) * 128], lhsT=ones,
                                                                   rhs=d1[:, h, :], start=True, stop=False),
                     reads=[cstA_t, d1_t], writes=[pe_t])
                s.op("pe", lambda e, h=h, pe_=pe_, nm=nm: e.matmul(pe_[:, h * 128:(h + 1) * 128], lhsT=ident, rhs=nm,
                                                                   start=False, stop=True),
                     reads=[cstA_t], writes=[pe_t])
            Et, Et_t = Epool.next()
            for h in hs:
                s.op("act", lambda e, h=h, Et=Et, pe_=pe_: e.activation(
                    out=Et[:, h, :], in_=pe_[:, h * 128:(h + 1) * 128], func=AF.Exp, bias=wcol("negc", h)),
                    reads=[pe_t, tl("negc")], writes=[Et_t])
            Es.append((Et, Et_t))
            yield
        (E1t, E1t_t), (E2t, E2t_t) = Es
        P, P_t = f4.next()
        R, R_t = f4.next()
        for h in hs:
            Gq, Gq_t = Gs[h // 2]
            s.op("pool", lambda e, h=h, Gq=Gq: e.tensor_tensor(out=attn[gi][0][:, h, :], in0=Gq[:, 128:256],
                                                               in1=E2t[:, h, :], op=ALU.mult),
                 reads=[Gq_t, E2t_t], writes=[attn[gi][1]])
            s.op("dve", lambda e, h=h, Gq=Gq, P=P: e.scalar_tensor_tensor(
                out=P[:, h, :], in0=Gq[:, 0:128], scalar=-1.0, in1=E1t[:, h, :], op0=ALU.mult, op1=ALU.mult),
                reads=[Gq_t, E1t_t], writes=[P_t])
        for h in hs:
            s.op("pool", lambda e, h=h, P=P, R=R: e.tensor_tensor(out=R[:, h, :], in0=P[:, h, :], in1=ident,
                                                                  op=ALU.add),
                 reads=[P_t, cstA_t], writes=[R_t])
        pp, pp_t = pss.next()
        for h in hs:
            s.op("pe", lambda e, h=h, P=P, pp=pp: e.matmul(pp[:, h * 128:(h + 1) * 128], lhsT=P[:, h, :], rhs=ident,
                                                           start=True, stop=True),
                 reads=[P_t, cstA_t], writes=[pp_t])
        PT, PT_t = f4.next()
        s.op("act", lambda e, PT=PT, pp=pp: e.copy(out=PT[:], in_=v4(pp[:])), reads=[pp_t], writes=[PT_t])
        yield
        for lvl in range(1, NLVL + 1):
            p1, p1_t = pss.next()
            for h in hs:
                s.op("pe", lambda e, h=h, p1=p1, P=P, PT=PT: e.matmul(p1[:, h * 128:(h + 1) * 128], lhsT=P[:, h, :],
                                                                      rhs=PT[:, h, :], start=True, stop=True),
                     reads=[P_t, PT_t], writes=[p1_t])
            PTn, PTn_t = f4.next()
            s.op("act", lambda e, PTn=PTn, p1=p1: e.copy(out=PTn[:], in_=v4(p1[:])), reads=[p1_t], writes=[PTn_t])
            if lvl < NLVL:
                p2, p2_t = pss.next()
                for h in hs:
                    s.op("pe", lambda e, h=h, p2=p2, P=P, PT=PT: e.matmul(p2[:, h * 128:(h + 1) * 128],
                                                                          lhsT=PT[:, h, :], rhs=P[:, h, :],
                                                                          start=True, stop=True),
                         reads=[P_t, PT_t], writes=[p2_t])
                Pn, Pn_t = f4.next()
                s.op("dve", lambda e, Pn=Pn, p2=p2: e.tensor_copy(out=Pn[:], in_=v4(p2[:])), reads=[p2_t],
                     writes=[Pn_t])
            p3, p3_t = pss.next()
            for h in hs:
                s.op("pe", lambda e, h=h, p3=p3, PTn=PTn, R=R: e.matmul(p3[:, h * 128:(h + 1) * 128],
                                                                        lhsT=PTn[:, h, :], rhs=R[:, h, :],
                                                                        start=True, stop=True),
                     reads=[PTn_t, R_t], writes=[p3_t])
            Rn, Rn_t = f4.next() if lvl < NLVL else TTb[gi]
            s.op("dve", lambda e, Rn=Rn, p3=p3, R=R: e.tensor_tensor(out=Rn[:], in0=v4(p3[:]), in1=R[:], op=ALU.add),
                 reads=[p3_t, R_t], writes=[Rn_t])
            R, R_t = Rn, Rn_t
            if lvl < NLVL:
                P, P_t, PT, PT_t = Pn, Pn_t, PTn, PTn_t
            yield
        pu, pu_t = pss.next()
        for h in hs:
            s.op("pe", lambda e, h=h, pu=pu: e.matmul(pu[:, h * 128:(h + 1) * 128], lhsT=TTb[gi][0][:, h, :],
                                                      rhs=bvb[gi][0][:, h, :], start=True, stop=True),
                 reads=[TTb[gi][1], bvb[gi][1]], writes=[pu_t])
        s.op("act", lambda e, pu=pu: e.copy(out=ub[gi][0][:], in_=v4(pu[:])), reads=[pu_t], writes=[ub[gi][1]])
        pw_, pw_t = pss.next()
        for h in hs:
            s.op("pe", lambda e, h=h, pw_=pw_: e.matmul(pw_[:, h * 128:(h + 1) * 128], lhsT=kbd[gi][0][:, h, :],
                                                        rhs=TTb[gi][0][:, h, :], start=True, stop=True),
                 reads=[TTb[gi][1], kbd[gi][1]], writes=[pw_t])
        s.op("act", lambda e, pw_=pw_: e.copy(out=wTb[gi][0][:], in_=v4(pw_[:])), reads=[pw_t], writes=[wTb[gi][1]])
        yield

    def scan_steps(gi, n, par, qk, qk_t, ob, ob_t):
        hs = range(4)
        gtz = GT[gi]
        wcol = lambda key, h: gtz[key][0][:, n * 4 + h:n * 4 + h + 1]
        tl = lambda key: gtz[key][1]
        ub, kdc, attn, wTb = ([None, None] for _ in range(4))
        ub[gi], kdc[gi], attn[gi], wTb[gi] = ub2[gi][par], kdc2[gi][par], attn2[gi][par], wTb2[gi][par]
        for b in (range(NSUB) if gi == 0 else range(NSUB - 1, -1, -1)):
            rb = slice(b * SC, (b + 1) * SC)
            P1, P1_t = pscan.next()
            for h in hs:
                s.op("pe", lambda e, h=h, P1=P1: e.matmul(P1[:, h * 128:(h + 1) * 128], lhsT=wTb[gi][0][:, h, :],
                                                          rhs=Sb[gi][0][:, h, :], start=True, stop=True),
                     reads=[wTb[gi][1], Sb[gi][1]], writes=[P1_t])
            s.op("dve", lambda e, P1=P1, rb=rb: e.tensor_tensor(out=vnew[gi][0][rb, :, :], in0=ub[gi][0][rb, :, :],
                                                                in1=v4(P1[rb, :]), op=ALU.subtract),
                 reads=[ub[gi][1], P1_t], writes=[vnew[gi][1]])
            yield
            P3, P3_t = pscan.next()
            for h in hs:
                s.op("pe", lambda e, h=h, P3=P3: e.matmul(P3[:, h * 128:(h + 1) * 128], lhsT=qk[:, h // 2, :],
                                                          rhs=Sb[gi][0][:, h, :], start=True, stop=True),
                     reads=[qk_t, Sb[gi][1]], writes=[P3_t])
            P2, P2_t = pscan.next()
            for h in hs:
                s.op("pe", lambda e, h=h, P2=P2, rb=rb: e.matmul(P2[:, h * 128:(h + 1) * 128],
                                                                 lhsT=attn[gi][0][rb, h, :], rhs=vnew[gi][0][rb, h, :],
                                                                 start=True, stop=True),
                     reads=[attn[gi][1], vnew[gi][1]], writes=[P2_t])
            s.op("act", lambda e, P2=P2, rb=rb: e.copy(out=tmpo[gi][0][rb, :, :], in_=v4(P2[rb, :])), reads=[P2_t],
                 writes=[tmpo[gi][1]])
            for h in hs:
                s.op("dve", lambda e, h=h, P3=P3, rb=rb: e.scalar_tensor_tensor(
                    out=ob[rb, h * 128:(h + 1) * 128], in0=P3[rb, h * 128:(h + 1) * 128],
                    scalar=wcol("expc", h)[rb, :], in1=tmpo[gi][0][rb, h, :], op0=ALU.mult, op1=ALU.add),
                    reads=[P3_t, tl("expc"), tmpo[gi][1]], writes=[ob_t])
            P4, P4_t = pscan.next()
            for h in hs:
                s.op("pe", lambda e, h=h, P4=P4, rb=rb: e.matmul(P4[:, h * 128:(h + 1) * 128],
                                                                 lhsT=kdc[gi][0][rb, h, :], rhs=vnew[gi][0][rb, h, :],
                                                                 start=True, stop=True),
                     reads=[kdc[gi][1], vnew[gi][1]], writes=[P4_t])
            for h in hs:
                s.op("dve", lambda e, h=h, P4=P4, b=b: e.scalar_tensor_tensor(
                    out=Sf[gi][0][:, h, :], in0=Sf[gi][0][:, h, :], scalar=gtz["cdb"][b][0][:, n * 4 + h:n * 4 + h + 1],
                    in1=P4[:, h * 128:(h + 1) * 128], op0=ALU.mult, op1=ALU.add),
                    reads=[Sf[gi][1], gtz["cdb"][b][1], P4_t], writes=[Sf[gi][1]])
            s.op("act", lambda e: e.copy(out=Sb[gi][0][:], in_=Sf[gi][0][:]), reads=[Sf[gi][1]], writes=[Sb[gi][1]])
            yield

    pend = []
    for k in range(NCH + 1):
        gens, outs = [], []
        for (gi, n, par, qk, qk_t) in pend:
            ob, ob_t = osb.next()
            gens.append(scan_steps(gi, n, par, qk, qk_t, ob, ob_t))
            outs.append((n, ob, ob_t))
        first = (k - 1) < NCH // 2
        pend = []
        if k < NCH:
            for gi in range(2):
                n = k if gi == 0 else NCH - 1 - k
                qk, qk_t = qkc[gi].next()
                cx.load("sp", qk[:], qknv[:, :, n * 128:(n + 1) * 128], qk_t, reads=[qkn_t])
                Gs = []
                for qh in range(2):
                    pg, pg_t = psw.next()
                    s.op("pe", lambda e, pg=pg, qh=qh, qk=qk: e.matmul(pg[:, 0:128], lhsT=qk[:, 2 + qh, :],
                                                                       rhs=qk[:, 2 + qh, :], start=True, stop=True),
                         reads=[qk_t], writes=[pg_t])
                    s.op("pe", lambda e, pg=pg, qh=qh, qk=qk: e.matmul(pg[:, 128:256], lhsT=qk[:, 2 + qh, :],
                                                                       rhs=qk[:, qh, :], start=True, stop=True),
                         reads=[qk_t], writes=[pg_t])
                    Gq, Gq_t = G.next()
                    s.op("act", lambda e, Gq=Gq, pg=pg: e.copy(out=Gq[:], in_=pg[:, 0:256]), reads=[pg_t],
                         writes=[Gq_t])
                    Gs.append((Gq, Gq_t))
                gens.append(prep_steps(gi, n, k % 2, qk, qk_t, Gs))
                pend.append((gi, n, k % 2, qk, qk_t))
        live = list(gens)
        while live:
            for gen in list(live):
                try:
                    next(gen)
                except StopIteration:
                    live.remove(gen)
        for (n, ob, ob_t) in outs:
            rows = slice(n * 128, (n + 1) * 128)
            if not first:
                oo, oo_t = oold.next()
                cx.load("sp", oo[:], src6[rows, :], oo_t, reads=[o_tiles[n]])
                s.op("pool", lambda e, ob=ob, oo=oo: e.tensor_tensor(out=ob[:], in0=ob[:], in1=oo[:], op=ALU.add),
                     reads=[oo_t, ob_t], writes=[ob_t])
            cx.store("sp", src6[rows, :], ob[:], ob_t, writes=[o_tiles[n], src6_t])


def dense_main(cx, dn, d, ncol_o, hw, ld_x, ld_o, tail):
    s = cx.s
    pc = Rot([cx.sb([128, 3, 512], F32, f"pc{i}") for i in range(2)])
    sgt = Rot([cx.sb([128, 512], F32, f"sg{i}") for i in range(2)])
    act, act_t = cx.sb([128, 4, TL], BF16, "actblk")
    pT, pT_t = cx.sb([128, 2, TL], BF16, "pT")
    onw, onw_t = cx.sb([128, 512], F32, "onw")
    cx.load("sp", onw[:], d["onw"][:, :], onw_t)

    def gain(name):
        g, g_t = dn.gains.next()
        cx.load("sp", g[:], d[name][:, :], g_t)
        return g, g_t

    def add_into_h(i, c, w, ps, ps_t):
        s.op("dve", lambda e: e.tensor_tensor(out=dn.h[:, i, c:c + w], in0=dn.h[:, i, c:c + w], in1=ps[:, 0:w],
                                              op=ALU.add), reads=[dn.h_t[i], ps_t], writes=[dn.h_t[i]])

    ld_x()
    for kh in range(ncol_o // D):
        for i in range(NT):
            xn, xn_t = dn.xn.next()
            for pcs in range(D // 512):
                c0 = kh * D + pcs * 512
                t3, t3_t = pc.next()
                ld_o(i, pcs, kh, t3, t3_t)
                s.op("act", lambda e, t3=t3: e.activation(out=t3[:, 2, :], in_=t3[:, 2, :], func=AF.Silu),
                     reads=[t3_t], writes=[t3_t])
                s.op("pool", lambda e, t3=t3: e.tensor_tensor(out=t3[:, 2, :], in0=t3[:, 2, :], in1=onw[:],
                                                              op=ALU.mult), reads=[t3_t, onw_t], writes=[t3_t])
                for sub in range(512 // hw):
                    cc = slice(sub * hw, (sub + 1) * hw)
                    r, r_t = dn.rstd(t3[:, 0, cc], [t3_t], hw)
                    s.op("dve", lambda e, t3=t3, cc=cc, r=r, xn=xn, pcs=pcs, sub=sub: e.scalar_tensor_tensor(
                        out=xn[:, pcs * 512 + sub * hw:pcs * 512 + (sub + 1) * hw], in0=t3[:, 0, cc], scalar=r,
                        in1=t3[:, 2, cc], op0=ALU.mult, op1=ALU.mult), reads=[t3_t, r_t], writes=[xn_t])
            dn.to_feat(xn, xn_t, i, D)
        dn.mm_tok(d["w_out"], KD, 0, D, add_into_h, r0=kh * D)

    g, g_t = gain("ffn_norm")
    dn.norm_to_feat(g, g_t)
    for fb in range(D_FF // 512):
        wg, wg_t = dn.load_w(d["ffn_w_in"], 0, KD, fb * 512, 512)
        wu, wu_t = dn.load_w(d["ffn_w_in"], 0, KD, D_FF + fb * 512, 512)
        wo, wo_t = dn.load_w(d["ffn_w_out"], fb * 512, 4, 0, D)
        for j in range(4):
            for th in range(TL // 512):
                ts_ = slice(th * 512, (th + 1) * 512)
                rt = [dn.aT_t[q] for q in range(th * 4, th * 4 + 4)]
                pg, pg_t = dn.psm.next()
                pu, pu_t = dn.psm.next()
                for (p_, p_t, w_, w_t) in ((pg, pg_t, wg, wg_t), (pu, pu_t, wu, wu_t)):
                    for k in range(KD):
                        s.op("pe", lambda e, p_=p_, w_=w_, k=k, j=j, ts_=ts_: e.matmul(
                            p_[:], lhsT=w_[:, k, j * 128:(j + 1) * 128], rhs=dn.aT[:, k, ts_], start=(k == 0),
                            stop=(k == KD - 1)), reads=rt + [w_t], writes=[p_t])
                sg, sg_t = sgt.next()
                s.op("act", lambda e, sg=sg, pg=pg: e.activation(out=sg[:], in_=pg[:], func=AF.Silu), reads=[pg_t],
                     writes=[sg_t])
                s.op("dve", lambda e, sg=sg, pu=pu, j=j, ts_=ts_: e.tensor_tensor(out=act[:, j, ts_], in0=sg[:],
                                                                                 in1=pu[:], op=ALU.mult),
                     reads=[sg_t, pu_t], writes=[act_t])
        for i in range(NT):
            for dt_ in range(D // 512):
                ps, ps_t = dn.psm.next()
                for j in range(4):
                    s.op("pe", lambda e, ps=ps, j=j, i=i, dt_=dt_, wo=wo: e.matmul(
                        ps[:], lhsT=act[:, j, i * 128:(i + 1) * 128], rhs=wo[:, j, dt_ * 512:(dt_ + 1) * 512],
                        start=(j == 0), stop=(j == 3)), reads=[act_t, wo_t], writes=[ps_t])
                add_into_h(i, dt_ * 512, 512, ps, ps_t)

    g, g_t = gain("ple_norm")
    dn.norm_to_feat(g, g_t)
    for i in range(NT):
        t3, t3_t = pc.next()
        cx.load("sp", t3[:, 0, 0:PLE], d["p"][i * 128:(i + 1) * 128, :], t3_t)
        xn, xn_t = dn.xn.next()
        s.op("act", lambda e, t3=t3, xn=xn: e.copy(out=xn[:, 0:PLE], in_=t3[:, 0, 0:PLE]), reads=[t3_t], writes=[xn_t])
        pt, pt_t = dn.pst.next()
        for j in range(2):
            s.op("pe", lambda e, j=j, pt=pt, xn=xn: e.transpose(out=pt[:, j * 128:(j + 1) * 128],
                                                                 in_=xn[:, j * 128:(j + 1) * 128],
                                                                 identity=dn.ident[:]),
                 reads=[xn_t, dn.ident_t], writes=[pt_t])
        s.op("act", lambda e, i=i, pt=pt: e.copy(out=pT[:, :, i * 128:(i + 1) * 128],
                                                 in_=pt[:, 0:256].rearrange("p (k t) -> p k t", t=128)),
             reads=[pt_t], writes=[pT_t])
    for c in range(0, D, 512):
        wgt, wgt_t = dn.load_w(d["ple_w_gate"], 0, KD, c, 512)
        wpj, wpj_t = dn.load_w(d["ple_w_proj"], 0, 2, c, 512)
        for i in range(NT):
            pg, pg_t = dn.psm.next()
            for k in range(KD):
                s.op("pe", lambda e, pg=pg, k=k, i=i, wgt=wgt: e.matmul(
                    pg[:], lhsT=dn.aT[:, k, i * 128:(i + 1) * 128], rhs=wgt[:, k, :], start=(k == 0),
                    stop=(k == KD - 1)), reads=[dn.aT_t[i], wgt_t], writes=[pg_t])
            pp, pp_t = dn.psm.next()
            for k in range(2):
                s.op("pe", lambda e, pp=pp, k=k, i=i, wpj=wpj: e.matmul(
                    pp[:], lhsT=pT[:, k, i * 128:(i + 1) * 128], rhs=wpj[:, k, :], start=(k == 0), stop=(k == 1)),
                    reads=[pT_t, wpj_t], writes=[pp_t])
            sg, sg_t = sgt.next()
            s.op("act", lambda e, sg=sg, pg=pg: e.activation(out=sg[:], in_=pg[:], func=AF.Sigmoid), reads=[pg_t],
                 writes=[sg_t])
            s.op("dve", lambda e, sg=sg, pp=pp: e.tensor_tensor(out=sg[:], in0=sg[:], in1=pp[:], op=ALU.mult),
                 reads=[sg_t, pp_t], writes=[sg_t])
            s.op("pool", lambda e, sg=sg, i=i, c=c: e.tensor_tensor(out=dn.h[:, i, c:c + 512], in0=dn.h[:, i, c:c + 512],
                                                                     in1=sg[:], op=ALU.add),
                 reads=[sg_t, dn.h_t[i]], writes=[dn.h_t[i]])

    tail(gain)


def build_fused(upto=None):
    cx = Ctx()
    s = cx.s
    pid = s.pid

    def di(name, shape, dt=F32):
        if name in PAD_KEYS:
            return cx.dram_in(name, [shape[0] + 1, shape[1]], dt)[0:shape[0], :]
        return cx.dram_in(name, shape, dt)
    x_d = di("x", [TL, D])
    p_d = [di("p0", [TL, PLE]), di("p1", [TL, PLE])]
    id_d = di("ident", [128, 128], BF16)
    g0_d = di("g0", [128, D])
    w0_d = di("w0", [D, GLA_PROJ])
    wgu_d = di("wgu", [2, GLA_R, GLA_DK])
    bb_d = di("bb", [128, 2 * GLA_DK])
    gcst_d = di("gcst", [128, 2 * 384])
    cw_d = di("cw", [128, 8 * CONV_W])
    hp_d = di("hp", [128, 4 * NCH * 4])
    dcst_d = di("dcst", [128, 2 * 8 * 128])
    L = []
    for l in range(2):
        ncol_o = GLA_VD if l == 0 else GDN_VD
        L.append({"onw": di(f"onw{l}", [128, 512]), "w_out": di(f"w_out{l}", [ncol_o, D]),
                  "ffn_norm": di(f"ffn_norm{l}", [128, D]), "ffn_w_in": di(f"ffn_w_in{l}", [D, 2 * D_FF]),
                  "ffn_w_out": di(f"ffn_w_out{l}", [D_FF, D]), "ple_norm": di(f"ple_norm{l}", [128, D]),
                  "ple_w_gate": di(f"ple_w_gate{l}", [D, D]), "ple_w_proj": di(f"ple_w_proj{l}", [PLE, D]),
                  "p": p_d[l], "next_norm": di(f"next_norm{l}", [128, D])})
    w1_d = di("w1", [D, GDN_PROJ])
    out_ds = [cx.dram_out(f"out{j}", [64, D]) for j in range(TL // 64)]

    src1, src1_t = cx.dram("src1", [G1R, TL])
    G1, G1_t = cx.dram("G1", [NCORES * G1R, TL])
    src2, src2_t = cx.dram("src2", [TL, GLA_KD + GLA_VD])
    G2, G2_t = cx.dram("G2", [T, GLA_KD + GLA_VD])
    og, og_t = cx.dram("og_loc", [TL, GLA_VD])
    src3, src3_t = cx.dram("src3", [T, GLA_DV // 2])
    G3, G3_t = cx.dram("G3", [NCORES * T, GLA_DV // 2])
    hloc, hloc_t = cx.dram("h_loc", [TL, D])
    src4, src4_t = cx.dram("src4", [GDN_CONV, TL])
    G4, G4_t = cx.dram("G4", [NCORES * GDN_CONV, TL])
    zloc, zloc_t = cx.dram("z_loc", [TL, GDN_VD])
    src5, src5_t = cx.dram("src5", [TL, 4 * GDN_VH])
    G5, G5_t = cx.dram("G5", [T, 4 * GDN_VH])
    qkn, qkn_t = cx.dram("qkn", [8 * 128, T], BF16)
    src6, src6_t = cx.dram("src6", [T, 4 * GDN_DV])
    G6, G6_t = cx.dram("G6", [NCORES * T, 4 * GDN_DV])

    def dbg_dump(name, src, src_t, shape):
        o_d = cx.dram_out(name, list(shape))
        rows, cols = shape
        tmp, tmp_t = cx.sb([128, cols], F32, "dbg")
        for r0 in range(0, rows, 128):
            m = min(128, rows - r0)
            cx.load("sp", tmp[0:m, :], src[r0:r0 + m, :], tmp_t, reads=[src_t])
            cx.store("sp", o_d[r0:r0 + m, :], tmp[0:m, :], tmp_t)

    dn = Dense(cx, id_d)
    g, g_t = dn.load_bc(g0_d, D, "g_bc")
    dn.load_h(x_d)
    dn.norm_to_feat(g, g_t)
    dense_feat_store(dn, w0_d, 0, 2 * GLA_KD, src1, src1_t, 0)
    dense_feat_store(dn, w0_d, 2 * GLA_KD + 2 * GLA_VD, GLA_PROJ, src1, src1_t, 2 * GLA_KD)
    dense_tok_store(dn, w0_d, GLA_KD, 2 * GLA_KD + GLA_VD, src2, src2_t, 0)
    dense_tok_store(dn, w0_d, 2 * GLA_KD + GLA_VD, 2 * GLA_KD + 2 * GLA_VD, og, og_t, 0)
    cx.allgather(src1[:, :], src1_t, G1[:, :], G1_t)
    cx.allgather(src2[:, :], src2_t, G2[:, :], G2_t)
    cx.phase_end()
    if upto == "A":
        dbg_dump("dbg_G2", G2[0:256, 0:512], G2_t, [256, 512])
        return cx.finish()

    gla_phase(cx, G1, G1_t, G2, G2_t, wgu_d, bb_d, gcst_d, src3, src3_t)
    cx.allgather(src3[:, :], src3_t, G3[:, :], G3_t)
    cx.phase_end()
    if upto == "B":
        dbg_dump("dbg_o", src3, src3_t, [T, GLA_DV // 2])
        return cx.finish()

    dn = Dense(cx, id_d)

    Lo, Lo_t = cx.dram("Lo", [TL, NCORES * (GLA_DV // 2)])
    G3v = G3.rearrange("(r t) d -> r t d", t=T)
    s.dma("sp", lambda e, sem: e.dma_start(out=Lo.rearrange("t (r d) -> r t d", d=GLA_DV // 2),
                                           in_=G3v[:, bass.ds(pid["sp"]["pTL"], TL), :]).then_inc(sem, 16),
          Lo_t, "loc", reads=[G3_t], writes=[Lo_t])

    def ld_o0(i, pcs, kh, t3, t3_t):
        def fn(e, sem):
            rows = slice(i * 128, (i + 1) * 128)
            e.dma_start(out=t3[:, 0, :], in_=Lo[rows, pcs * 512:(pcs + 1) * 512]).then_inc(sem, 16)
            e.dma_start(out=t3[:, 2, :], in_=og[rows, pcs * 512:(pcs + 1) * 512]).then_inc(sem, 16)
        s.dma("sp", fn, t3_t, "ld", reads=[Lo_t, og_t], writes=[t3_t], n=2)

    def tail0(gain):
        for i in range(NT):
            cx.store("sp", hloc[i * 128:(i + 1) * 128, :], dn.h[:, i, :], dn.h_t[i], writes=[hloc_t])
        g, g_t = gain("next_norm")
        dn.norm_to_feat(g, g_t)
        dense_feat_store(dn, w1_d, 0, GDN_CONV, src4, src4_t, 0)
        dense_tok_store(dn, w1_d, GDN_CONV, GDN_CONV + GDN_VD, zloc, zloc_t, 0)
        dense_tok_store(dn, w1_d, GDN_CONV + GDN_VD, GDN_PROJ, src5, src5_t, 0)
    dense_main(cx, dn, L[0], GLA_VD, GLA_DV, lambda: dn.load_h(x_d), ld_o0, tail0)
    cx.allgather(src4[:, :], src4_t, G4[:, :], G4_t)
    cx.allgather(src5[:, :], src5_t, G5[:, :], G5_t)
    cx.phase_end()
    if upto == "C":
        dbg_dump("dbg_h", hloc, hloc_t, [TL, D])
        return cx.finish()

    gdn_phase(cx, G4, G4_t, G5, G5_t, cw_d, hp_d, dcst_d, id_d, qkn, qkn_t, src6, src6_t)
    cx.allgather(src6[:, :], src6_t, G6[:, :], G6_t)
    cx.phase_end()
    if upto == "D":
        dbg_dump("dbg_o", src6, src6_t, [T, 4 * GDN_DV])
        return cx.finish()

    dn = Dense(cx, id_d)

    def ld_x1():
        for i in range(NT):
            cx.load("sp", dn.h[:, i, :], hloc[i * 128:(i + 1) * 128, :], dn.h_t[i], reads=[hloc_t])

    Lo6, Lo6_t = cx.dram("Lo6", [TL, NCORES * 4 * GDN_DV])
    G6v = G6.rearrange("(r t) d -> r t d", t=T)
    s.dma("sp", lambda e, sem: e.dma_start(out=Lo6.rearrange("t (r d) -> r t d", d=4 * GDN_DV),
                                           in_=G6v[:, bass.ds(pid["sp"]["pTL"], TL), :]).then_inc(sem, 16),
          Lo6_t, "loc", reads=[G6_t], writes=[Lo6_t])

    def ld_o1(i, pcs, kh, t3, t3_t):
        def fn(e, sem):
            rows = slice(i * 128, (i + 1) * 128)
            c0 = kh * D + pcs * 512
            e.dma_start(out=t3[:, 0, :], in_=Lo6[rows, c0:c0 + 512]).then_inc(sem, 16)
            e.dma_start(out=t3[:, 2, :], in_=zloc[rows, c0:c0 + 512]).then_inc(sem, 16)
        s.dma("sp", fn, t3_t, "ld", reads=[Lo6_t, zloc_t], writes=[t3_t], n=2)

    def tail1(gain):
        g, g_t = gain("next_norm")
        for i in range(NT):
            r, r_t = dn.rstd(dn.h[:, i, :], [dn.h_t[i]], D)
            s.op("dve", lambda e, i=i, r=r: e.scalar_tensor_tensor(
                out=dn.h[:, i, :], in0=dn.h[:, i, :], scalar=r, in1=g[:], op0=ALU.mult, op1=ALU.mult),
                reads=[dn.h_t[i], r_t, g_t], writes=[dn.h_t[i]])
            for hf in range(2):
                cx.store("sp", out_ds[2 * i + hf][:, :], dn.h[hf * 64:(hf + 1) * 64, i, :], dn.h_t[i])
    dense_main(cx, dn, L[1], GDN_VD, GDN_DV, ld_x1, ld_o1, tail1)
    return cx.finish()


_CACHE = {}
PAD_KEYS = {"g0", "w0", "dcst", "w1"} | {f"{k}{l}" for l in range(2) for k in (
    "w_out", "ffn_norm", "ffn_w_in", "ffn_w_out", "ple_norm", "ple_w_gate", "ple_w_proj", "next_norm")}


def _c(a):
    return np.ascontiguousarray(a, dtype=np.float32)


def make_in_maps(x, p, mixer_norm, gla_w_in, gla_w_gate_up, gla_b_gate, gla_out_norm, gla_w_out,
                 gdn_w_in, gdn_conv, gdn_a_log, gdn_dt_bias, gdn_out_norm, gdn_w_out,
                 ffn_norm, ffn_w_in, ffn_w_out, ple_norm, ple_w_gate, ple_w_proj, final_norm):
    f = lambda a: np.asarray(a, dtype=np.float32)
    x2 = f(x)[0]
    p = f(p)
    w1 = f(gdn_w_in)[0]
    ba_cols = GDN_CONV + GDN_VD + np.arange(4 * GDN_VH).reshape(2, 2, 8, 4).transpose(2, 0, 1, 3).reshape(-1)
    w1p = _c(np.concatenate([w1[:, :GDN_CONV + GDN_VD], w1[:, ba_cols]], axis=1))
    shared = {"ident": bf16_ident(), "g0": rep128(f(mixer_norm)[0]), "w0": _c(f(gla_w_in)[0]),
              "gcst": _c(np.concatenate([gla_consts(False), gla_consts(True)], axis=1)),
              "dcst": _c(np.concatenate([gdn_consts(False), gdn_consts(True)], axis=1)), "w1": w1p}
    norms_next = [f(mixer_norm)[1], f(final_norm)]
    onw = [f(gla_out_norm)[0], np.tile(f(gdn_out_norm)[0], 512 // GDN_DV)]
    wout = [f(gla_w_out)[0], f(gdn_w_out)[0]]
    for l in range(2):
        shared.update({f"onw{l}": rep128(onw[l]), f"w_out{l}": _c(wout[l]), f"ffn_norm{l}": rep128(f(ffn_norm)[l]),
                       f"ffn_w_in{l}": _c(f(ffn_w_in)[l]), f"ffn_w_out{l}": _c(f(ffn_w_out)[l]),
                       f"ple_norm{l}": rep128(f(ple_norm)[l]), f"ple_w_gate{l}": _c(f(ple_w_gate)[l]),
                       f"ple_w_proj{l}": _c(f(ple_w_proj)[l]), f"next_norm{l}": rep128(norms_next[l])})
    conv = f(gdn_conv)[0]
    maps = []
    for c in range(NCORES):
        m = dict(shared)
        for k in PAD_KEYS:
            a = shared[k]
            m[k] = np.concatenate([a, np.full((1, a.shape[1]), float(c), np.float32)], axis=0)
        hh = c // 2
        m["x"] = _c(x2[c * TL:(c + 1) * TL])
        m["p0"] = _c(p[0, 0, c * TL:(c + 1) * TL])
        m["p1"] = _c(p[1, 0, c * TL:(c + 1) * TL])
        m["wgu"] = _c(f(gla_w_gate_up)[0][:, :, hh * GLA_DK:(hh + 1) * GLA_DK])
        m["bb"] = rep128(np.concatenate([f(gla_b_gate)[0, z, hh * GLA_DK:(hh + 1) * GLA_DK] for z in range(2)]))
        cols = np.concatenate([np.arange(2 * c * 128, (2 * c + 2) * 128),
                               GDN_KD + np.arange(2 * c * 128, (2 * c + 2) * 128),
                               2 * GDN_KD + np.arange(4 * c * 128, (4 * c + 4) * 128)])
        m["cw"] = _c(conv[:, cols].T.reshape(8, 128, CONV_W).transpose(1, 0, 2).reshape(128, 8 * CONV_W))
        hsl = slice(4 * c, 4 * c + 4)
        hp = np.concatenate([np.tile(f(a)[0, z, hsl], NCH) for z in range(2) for a in (gdn_a_log, gdn_dt_bias)])
        m["hp"] = rep128(hp)
        maps.append(m)
    return maps


def kernel(**inputs):
    if "nc" not in _CACHE:
        _CACHE["nc"] = build_fused()
    maps = make_in_maps(**inputs)
    res = run_bass_kernel_spmd(_CACHE["nc"], maps, core_ids=list(range(NCORES))).results
    out = np.concatenate([r[f"out{j}"] for r in res for j in range(TL // 64)], axis=0)
    return out[None].astype(np.float32)
```
